# Optimizing a Trainium2 kernel written in Bass

```python
import jax, jax.numpy as jnp
from jax import lax
import numpy as np

D_MODEL = 1024
BATCH = 8
SEQ = 2048
DEPTH = 2

GRID_W = 64
NA_HEADS = 4
NA_HEAD_DIM = 64
NA_KH = 8
NA_KW = 16
NA_QB = 16
MLA_HEADS = 4
MLA_Q_LORA = 256
MLA_KV_LORA = 128
MLA_NOPE = 64
MLA_ROPE = 32
MLA_V = 64
GQA_HEADS = 8
GQA_KV_HEADS = 2
GQA_HEAD_DIM = 64
N_BRANCHES = 3
FFN_HIDDEN = -(-8 * D_MODEL // (3 * 256)) * 256
Q_BLOCK = 128
ROPE_THETA = 10000.0
AXIAL_THETA = 10000.0
NORM_EPS = 1e-6
N_MOD = 6

kernel_name = "hybrid_na_mla_gqa_encoder"


def _in_splits():
    return ([NA_HEADS * NA_HEAD_DIM] * 3
            + [MLA_Q_LORA, MLA_KV_LORA, MLA_ROPE]
            + [GQA_HEADS * GQA_HEAD_DIM, GQA_KV_HEADS * GQA_HEAD_DIM, GQA_KV_HEADS * GQA_HEAD_DIM]
            + [D_MODEL] * N_BRANCHES)


def _in_cols():
    return int(sum(_in_splits()))


def rms_norm(x, g):
    xf = x.astype(jnp.float32)
    y = xf * lax.rsqrt(jnp.mean(xf * xf, axis=-1, keepdims=True) + NORM_EPS)
    return (y * g.astype(jnp.float32)).astype(x.dtype)


def rope(x, pos, theta):
    d = x.shape[-1]
    half = d // 2
    inv_freq = 1.0 / (theta ** (jnp.arange(half, dtype=jnp.float32) / half))
    ang = pos[:, None] * inv_freq[None, :]
    ang = ang.reshape((ang.shape[0],) + (1,) * (x.ndim - 3) + (half,))
    cos, sin = jnp.cos(ang), jnp.sin(ang)
    xf = x.astype(jnp.float32)
    x1, x2 = xf[..., :half], xf[..., half:]
    return jnp.concatenate([x1 * cos - x2 * sin, x2 * cos + x1 * sin], axis=-1).astype(x.dtype)


def axial_rope(x, row, col):
    h = x.shape[-1] // 2
    return jnp.concatenate([rope(x[..., :h], row, AXIAL_THETA), rope(x[..., h:], col, AXIAL_THETA)], axis=-1)


def blocked_attention(q, k, v, scale):
    B, S, KVH, G, dk = q.shape
    dv = v.shape[-1]
    nb = S // Q_BLOCK
    qb = jnp.moveaxis(q.reshape(B, nb, Q_BLOCK, KVH, G, dk), 1, 0)

    def one_block(qi):
        s = jnp.einsum('bqkgd,bskd->bkgqs', qi, k, preferred_element_type=jnp.float32) * scale
        p = jax.nn.softmax(s, axis=-1)
        return jnp.einsum('bkgqs,bskd->bqkgd', p.astype(v.dtype), v)

    o = lax.map(one_block, qb)
    return jnp.moveaxis(o, 0, 1).reshape(B, S, KVH * G * dv)


def neighborhood_attention(q, k, v, rpb):
    B, S, H, d = q.shape
    rows = S // GRID_W
    kh = min(NA_KH, rows)
    kw = NA_KW
    n_cb = GRID_W // NA_QB
    band = NA_QB + kw
    scale = d ** -0.5
    qg = q.reshape(B, rows, GRID_W, H, d)
    kg = k.reshape(B, rows, GRID_W, H, d)
    vg = v.reshape(B, rows, GRID_W, H, d)

    cb_start = jnp.clip(jnp.arange(n_cb) * NA_QB - kw // 2, 0, GRID_W - band)
    band_cols = cb_start[:, None] + jnp.arange(band)[None, :]
    q_cols = (jnp.arange(n_cb) * NA_QB)[:, None] + jnp.arange(NA_QB)[None, :]
    win_start = jnp.clip(q_cols - kw // 2, 0, GRID_W - kw)
    bc = band_cols[:, None, :]
    col_valid = (bc >= win_start[..., None]) & (bc < win_start[..., None] + kw)
    col_off = jnp.clip(bc - q_cols[..., None], -(kw - 1), kw - 1) + (kw - 1)
    rpb_cols = rpb[:, :, col_off]

    def one_row(r):
        rs = jnp.clip(r - kh // 2, 0, rows - kh)
        k_rows = lax.dynamic_slice_in_dim(kg, rs, kh, axis=1)
        v_rows = lax.dynamic_slice_in_dim(vg, rs, kh, axis=1)
        k_band = k_rows[:, :, band_cols]
        v_band = v_rows[:, :, band_cols]
        q_row = lax.dynamic_index_in_dim(qg, r, axis=1, keepdims=False).reshape(B, n_cb, NA_QB, H, d)
        s = jnp.einsum('bnqhd,bknjhd->bhnqkj', q_row, k_band, preferred_element_type=jnp.float32) * scale
        row_off = rs + jnp.arange(kh) - r + (NA_KH - 1)
        bias = jnp.transpose(rpb_cols[:, row_off], (0, 2, 3, 1, 4))
        s = s + bias[None].astype(jnp.float32)
        s = jnp.where(col_valid[None, None, :, :, None, :], s, -jnp.inf)
        p = jax.nn.softmax(s.reshape(B, H, n_cb, NA_QB, kh * band), axis=-1)
        p = p.reshape(B, H, n_cb, NA_QB, kh, band)
        o = jnp.einsum('bhnqkj,bknjhd->bnqhd', p.astype(v.dtype), v_band)
        return o.reshape(B, GRID_W, H, d)

    out = lax.map(one_row, jnp.arange(rows))
    return jnp.moveaxis(out, 0, 1).reshape(B, S, H * d)


def hybrid_layer(x, c, ada_w, ada_b, norm_mix_g, norm_ffn_g, w_in, na_rpb,
                 mla_q_norm_g, mla_kv_norm_g, mla_w_uq, mla_w_ukv, gqa_q_norm_g, gqa_k_norm_g,
                 w_br_na, w_br_mla, w_br_gqa, w_out, ffn_w_gate_up, ffn_w_down, pos_t, pos_row, pos_col):
    B, S, D = x.shape
    mod = jax.nn.silu(c) @ ada_w + ada_b
    sh_m, sc_m, g_m, sh_f, sc_f, g_f = [m[:, None, :] for m in jnp.split(mod, N_MOD, axis=-1)]

    h = rms_norm(x, norm_mix_g) * (1 + sc_m) + sh_m
    proj = h @ w_in
    idx = [int(i) for i in np.cumsum(_in_splits())[:-1]]
    (qa, ka, va, cq, ckv, kr, qc, kc, vc, ga, gb, gc) = jnp.split(proj, idx, axis=-1)

    o_na = neighborhood_attention(qa.reshape(B, S, NA_HEADS, NA_HEAD_DIM),
                                  ka.reshape(B, S, NA_HEADS, NA_HEAD_DIM),
                                  va.reshape(B, S, NA_HEADS, NA_HEAD_DIM), na_rpb)

    q_b = (rms_norm(cq, mla_q_norm_g) @ mla_w_uq).reshape(B, S, MLA_HEADS, MLA_NOPE + MLA_ROPE)
    q_nope, q_pe = q_b[..., :MLA_NOPE], rope(q_b[..., MLA_NOPE:], pos_t, ROPE_THETA)
    kv_b = (rms_norm(ckv, mla_kv_norm_g) @ mla_w_ukv).reshape(B, S, MLA_HEADS, MLA_NOPE + MLA_V)
    k_nope, v_b = kv_b[..., :MLA_NOPE], kv_b[..., MLA_NOPE:]
    k_pe = jnp.broadcast_to(rope(kr, pos_t, ROPE_THETA)[:, :, None, :], (B, S, MLA_HEADS, MLA_ROPE))
    q_full = jnp.concatenate([q_nope, q_pe], axis=-1)[:, :, :, None, :]
    k_full = jnp.concatenate([k_nope, k_pe], axis=-1)
    o_mla = blocked_attention(q_full, k_full, v_b, (MLA_NOPE + MLA_ROPE) ** -0.5)

    q_c = axial_rope(rms_norm(qc.reshape(B, S, GQA_HEADS, GQA_HEAD_DIM), gqa_q_norm_g), pos_row, pos_col)
    k_c = axial_rope(rms_norm(kc.reshape(B, S, GQA_KV_HEADS, GQA_HEAD_DIM), gqa_k_norm_g), pos_row, pos_col)
    v_c = vc.reshape(B, S, GQA_KV_HEADS, GQA_HEAD_DIM)
    q_c = q_c.reshape(B, S, GQA_KV_HEADS, GQA_HEADS // GQA_KV_HEADS, GQA_HEAD_DIM)
    o_gqa = blocked_attention(q_c, k_c, v_c, GQA_HEAD_DIM ** -0.5)

    merged = (jax.nn.sigmoid(ga) * (o_na @ w_br_na)
              + jax.nn.sigmoid(gb) * (o_mla @ w_br_mla)
              + jax.nn.sigmoid(gc) * (o_gqa @ w_br_gqa))
    x = x + g_m * (merged @ w_out)

    h = rms_norm(x, norm_ffn_g) * (1 + sc_f) + sh_f
    gate, up = jnp.split(h @ ffn_w_gate_up, 2, axis=-1)
    x = x + g_f * ((jax.nn.silu(gate) * up) @ ffn_w_down)
    return x


def setup_inputs(seed: int = 0) -> dict:
    key = jax.random.key(seed)
    ks = jax.random.split(key, 24)
    f32 = jnp.float32

    def w(k, shape, fan_in):
        return jax.random.normal(k, shape, f32) * fan_in ** -0.5

    def gain(k, shape):
        return 1.0 + 0.02 * jax.random.normal(k, shape, f32)

    L, D = DEPTH, D_MODEL
    return {
        "x": jax.random.normal(ks[0], (BATCH, SEQ, D), f32),
        "c": jax.random.normal(ks[1], (BATCH, D), f32),
        "ada_w": w(ks[2], (L, D, N_MOD * D), D),
        "ada_b": 0.02 * jax.random.normal(ks[3], (L, N_MOD * D), f32),
        "norm_mix_g": gain(ks[4], (L, D)),
        "norm_ffn_g": gain(ks[5], (L, D)),
        "w_in": w(ks[6], (L, D, _in_cols()), D),
        "na_rpb": 0.1 * jax.random.normal(ks[7], (L, NA_HEADS, 2 * NA_KH - 1, 2 * NA_KW - 1), f32),
        "mla_q_norm_g": gain(ks[8], (L, MLA_Q_LORA)),
        "mla_kv_norm_g": gain(ks[9], (L, MLA_KV_LORA)),
        "mla_w_uq": w(ks[10], (L, MLA_Q_LORA, MLA_HEADS * (MLA_NOPE + MLA_ROPE)), MLA_Q_LORA),
        "mla_w_ukv": w(ks[11], (L, MLA_KV_LORA, MLA_HEADS * (MLA_NOPE + MLA_V)), MLA_KV_LORA),
        "gqa_q_norm_g": gain(ks[12], (L, GQA_HEAD_DIM)),
        "gqa_k_norm_g": gain(ks[13], (L, GQA_HEAD_DIM)),
        "w_br_na": w(ks[14], (L, NA_HEADS * NA_HEAD_DIM, D), NA_HEADS * NA_HEAD_DIM),
        "w_br_mla": w(ks[15], (L, MLA_HEADS * MLA_V, D), MLA_HEADS * MLA_V),
        "w_br_gqa": w(ks[16], (L, GQA_HEADS * GQA_HEAD_DIM, D), GQA_HEADS * GQA_HEAD_DIM),
        "w_out": w(ks[17], (L, D, D), D),
        "ffn_w_gate_up": w(ks[18], (L, D, 2 * FFN_HIDDEN), D),
        "ffn_w_down": w(ks[19], (L, FFN_HIDDEN, D), FFN_HIDDEN),
        "final_norm_g": gain(ks[20], (D,)),
    }


def reference(x, c, ada_w, ada_b, norm_mix_g, norm_ffn_g, w_in, na_rpb, mla_q_norm_g, mla_kv_norm_g,
              mla_w_uq, mla_w_ukv, gqa_q_norm_g, gqa_k_norm_g, w_br_na, w_br_mla, w_br_gqa, w_out,
              ffn_w_gate_up, ffn_w_down, final_norm_g):
    S = x.shape[1]
    t = jnp.arange(S, dtype=jnp.int32)
    pos_t = t.astype(jnp.float32)
    pos_row = (t // GRID_W).astype(jnp.float32)
    pos_col = (t % GRID_W).astype(jnp.float32)
    for l in range(DEPTH):
        x = hybrid_layer(x, c, ada_w[l], ada_b[l], norm_mix_g[l], norm_ffn_g[l], w_in[l], na_rpb[l],
                         mla_q_norm_g[l], mla_kv_norm_g[l], mla_w_uq[l], mla_w_ukv[l],
                         gqa_q_norm_g[l], gqa_k_norm_g[l], w_br_na[l], w_br_mla[l], w_br_gqa[l], w_out[l],
                         ffn_w_gate_up[l], ffn_w_down[l], pos_t, pos_row, pos_col)
    return rms_norm(x, final_norm_g)
```

```python
import contextlib
import numpy as np
import concourse.bass as bass
import concourse.mybir as mybir
from concourse.bass_utils import run_bass_kernel_spmd

F32 = mybir.dt.float32
BF16 = mybir.dt.bfloat16
AF = mybir.ActivationFunctionType
ALU = mybir.AluOpType

S_ = 2048
D_ = 1024
KC = 8
NTC = 4
L_ = 2
HID = 2816
NJ = 22
EPS = 1e-6
NEG = -30000.0

COMPUTE = ("pe", "act", "dve", "pool")


class Res:
    __slots__ = ("name", "last_w", "readers")

    def __init__(self, name):
        self.name = name
        self.last_w = None
        self.readers = []


class Ins:
    __slots__ = ("eng", "fn", "deps", "inc", "is_dma", "dsem", "dval", "tokval")

    def __init__(self, eng, fn):
        self.eng = eng
        self.fn = fn
        self.deps = []
        self.inc = False
        self.is_dma = False
        self.dsem = None
        self.dval = 0
        self.tokval = 0


class _Rec:
    def __init__(self):
        self.calls = []

    def __getattr__(self, name):
        def call(*a, **k):
            self.calls.append((name, a, k))
            return None
        return call


class Sched:
    def __init__(self, nc):
        self.nc = nc
        self.streams = {e: [] for e in ("pe", "act", "dve", "pool", "sp")}
        self.pending = {e: [] for e in self.streams}
        self.dma_sems = []
        self.resd = {}

    def R(self, *key):
        r = self.resd.get(key)
        if r is None:
            r = Res(key)
            self.resd[key] = r
        return r

    def new_dma_sem(self):
        rec = [None, 0, len(self.dma_sems)]
        self.dma_sems.append(rec)
        return rec

    def _add(self, eng, fn, reads, writes, is_dma=False, dsem=None):
        rec = _Rec()
        fn(rec)
        assert len(rec.calls) == 1
        name, a, k = rec.calls[0]
        fn = (lambda e, name=name, a=a, k=k: getattr(e, name)(*a, **k))
        ins = Ins(eng, fn)
        ins.is_dma = is_dma
        deps = []
        for r in reads:
            if r.last_w is not None:
                deps.append(r.last_w)
        for w in writes:
            if w.last_w is not None:
                deps.append(w.last_w)
            deps.extend(w.readers)
        deps.extend(self.pending[eng])
        self.pending[eng] = []
        seen = set()
        for d in deps:
            if id(d) in seen:
                continue
            seen.add(id(d))
            if (not d.is_dma) and d.eng == eng and eng in ("pe", "sp"):
                continue
            d.inc = True
            ins.deps.append((d, d.dsem[1] if d.is_dma else 0))
        if is_dma:
            ins.dsem = dsem
            dsem[1] += 16
            ins.dval = dsem[1]
            ins.inc = True
        for r in reads:
            r.readers.append(ins)
        for w in writes:
            w.last_w = ins
            w.readers = []
        self.streams[eng].append(ins)
        return ins

    def op(self, eng, fn, reads=(), writes=()):
        return self._add(eng, fn, list(reads), list(writes))

    def dma(self, queue, fn, reads=(), writes=(), sem=None):
        return self._add(queue, fn, list(reads), list(writes), is_dma=True, dsem=sem)

    def barrier(self):
        toks = []
        for e in self.streams:
            for ins in reversed(self.streams[e]):
                if not ins.is_dma:
                    if e != "sp":
                        toks.append(ins)
                    break
        lastd = {}
        for e in self.streams:
            for ins in self.streams[e]:
                if ins.is_dma:
                    lastd[ins.dsem[2]] = ins
        toks.extend(lastd.values())
        for t in toks:
            t.inc = True
        for e in self.streams:
            self.pending[e] = list(toks)

    def emit(self):
        nc = self.nc
        with contextlib.ExitStack() as es:
            esem = {}
            for e in COMPUTE:
                esem[e] = es.enter_context(nc.semaphore("s_" + e))
            for rec in self.dma_sems:
                rec[0] = es.enter_context(nc.semaphore("d%d" % rec[2]))
            for e in COMPUTE:
                nc.gpsimd.sem_clear(esem[e])
            for rec in self.dma_sems:
                nc.gpsimd.sem_clear(rec[0])
            for e in COMPUTE:
                k = 0
                for ins in self.streams[e]:
                    if ins.is_dma:
                        continue
                    if ins.inc:
                        k += 1
                        ins.tokval = k
            block = es.enter_context(nc.Block())
            streams = self.streams
            final_dma = [(rec[0], rec[1]) for rec in self.dma_sems if rec[1] > 0]

            def run(ename, engobj):
                known = {}
                for ins in streams[ename]:
                    for (d, need) in ins.deps:
                        if d.is_dma:
                            key = ("d", d.dsem[2]); sem = d.dsem[0]; val = need
                        else:
                            key = ("e", d.eng); sem = esem[d.eng]; val = d.tokval
                        if known.get(key, 0) >= val:
                            continue
                        known[key] = val
                        engobj.wait_ge(sem, val)
                    bi = ins.fn(engobj)
                    if ins.is_dma:
                        bi.then_inc(ins.dsem[0], 16)
                    elif ins.inc:
                        bi.then_inc(esem[ename], 1)
                if ename == "sp":
                    for sem, val in final_dma:
                        engobj.wait_ge(sem, val)

            @block.tensor
            def _(eng):
                run("pe", eng)

            @block.scalar
            def _(eng):
                run("act", eng)

            @block.vector
            def _(eng):
                run("dve", eng)

            @block.gpsimd
            def _(eng):
                run("pool", eng)

            @block.sync
            def _(eng):
                run("sp", eng)


def _kc_major(w, cols=None):
    K, M = w.shape
    nk = K // 128
    return np.ascontiguousarray(w.reshape(nk, 128, M).transpose(1, 0, 2).reshape(128, nk * M))


def _swap_idx(n_heads, dim, groups):
    idx = np.arange(n_heads * dim).reshape(n_heads, dim).copy()
    out = idx.copy()
    for (s, sz) in groups:
        h = sz // 2
        out[:, s:s + h] = idx[:, s + h:s + sz]
        out[:, s + h:s + sz] = idx[:, s:s + h]
    return out.reshape(-1)


class Packer:
    def __init__(self):
        self.parts = []
        self.off = 0
        self.groups = {}

    def add(self, name, arr):
        arr = np.ascontiguousarray(arr, dtype=np.float32)
        assert arr.shape[0] == 128
        n = arr.shape[1]
        assert n <= 4096, (name, n)
        self.groups[name] = (self.off, n)
        self.parts.append(arr)
        self.off += n


def group_layout():
    off = 0
    g = {}

    def add(name, n):
        nonlocal off
        g[name] = (off, n)
        off += n
    for l in range(L_):
        for i in range(12):
            add(("ada", l, i), 4096)
        add(("naqk", l), 4096)
        add(("nav", l), 2048)
        add(("mlaA", l), 3072)
        add(("mlaB", l), 1536)
        add(("mlaup", l), 2048)
        add(("gqa", l), 4096)
        add(("gqb", l), 4096)
        add(("gk", l), 4096)
        add(("gv", l), 1024)
        for fo in range(8):
            add(("mrg", l, fo), 4096)
        for fo in range(8):
            add(("wout", l, fo), 1024)
        for jj in range(11):
            add(("gu", l, jj), 4096)
        for fo in range(8):
            add(("down", l, fo), 2816)
    return g, off


def pack_weights(inp):
    P = Packer()
    sw_mla = _swap_idx(1, 32, [(0, 32)])
    sw_g = _swap_idx(1, 64, [(0, 32), (32, 32)])
    for l in range(L_):
        ada = inp["ada_w"][l]
        for i in range(12):
            P.add(("ada", l, i), _kc_major(ada[:, i * 512:(i + 1) * 512]))
        w_in = inp["w_in"][l]
        o = 0
        qa = w_in[:, 0:256]; ka = w_in[:, 256:512]; va = w_in[:, 512:768]
        cq = w_in[:, 768:1024]; ckv = w_in[:, 1024:1152]; kr = w_in[:, 1152:1184]
        qc = w_in[:, 1184:1696]; kc_ = w_in[:, 1696:1824]; vc = w_in[:, 1824:1952]
        ga = w_in[:, 1952:2976]; gb = w_in[:, 2976:4000]; gc = w_in[:, 4000:5024]
        P.add(("naqk", l), _kc_major(np.concatenate([qa, ka], axis=1)))
        P.add(("nav", l), _kc_major(va))
        P.add(("mlaA", l), _kc_major(np.concatenate([cq, ckv], axis=1)))
        krA = np.concatenate([kr, kr, kr], axis=1)
        krB = np.concatenate([kr, kr, kr[:, sw_mla]], axis=1)
        P.add(("mlaB", l), _kc_major(np.concatenate([krA, krB], axis=1)))
        uq = inp["mla_w_uq"][l]
        ukv = inp["mla_w_ukv"][l]
        uq_cols = []
        for h in range(4):
            qh = uq[:, h * 96:(h + 1) * 96]
            qh_sw = np.concatenate([qh[:, 0:64], qh[:, 64:96][:, sw_mla]], axis=1)
            uq_cols += [qh, qh_sw]
        uqp = _kc_major(np.concatenate(uq_cols, axis=1))
        kn = np.concatenate([ukv[:, h * 128:h * 128 + 64] for h in range(4)], axis=1)
        vv = np.concatenate([ukv[:, h * 128 + 64:h * 128 + 128] for h in range(4)], axis=1)
        P.add(("mlaup", l), np.concatenate([uqp, kn, vv], axis=1))
        qc_sw = np.concatenate([qc[:, h * 64:(h + 1) * 64][:, sw_g] for h in range(8)], axis=1)
        P.add(("gqa", l), _kc_major(np.concatenate([qc[:, 0:256], qc_sw[:, 0:256]], axis=1)))
        P.add(("gqb", l), _kc_major(np.concatenate([qc[:, 256:512], qc_sw[:, 256:512]], axis=1)))
        k0 = kc_[:, 0:64]; k1 = kc_[:, 64:128]
        P.add(("gk", l), _kc_major(np.concatenate([k0, k0, k1, k1, k0[:, sw_g], k0[:, sw_g], k1[:, sw_g], k1[:, sw_g]], axis=1)))
        P.add(("gv", l), _kc_major(vc))
        bna = inp["w_br_na"][l]; bml = inp["w_br_mla"][l]; bgq = inp["w_br_gqa"][l]
        for fo in range(8):
            cs = slice(fo * 128, (fo + 1) * 128)
            P.add(("mrg", l, fo), np.concatenate([
                _kc_major(ga[:, cs]), _kc_major(gb[:, cs]), _kc_major(gc[:, cs]),
                _kc_major(bna[:, cs]), _kc_major(bml[:, cs]), _kc_major(bgq[:, cs])], axis=1))
        wo = inp["w_out"][l]
        for fo in range(8):
            P.add(("wout", l, fo), _kc_major(wo[:, fo * 128:(fo + 1) * 128]))
        gu = inp["ffn_w_gate_up"][l]
        for jj in range(11):
            parts = []
            for j in (2 * jj, 2 * jj + 1):
                parts.append(_kc_major(gu[:, j * 128:(j + 1) * 128]))
                parts.append(_kc_major(gu[:, HID + j * 128:HID + (j + 1) * 128]))
            P.add(("gu", l, jj), np.concatenate(parts, axis=1))
        wd = inp["ffn_w_down"][l]
        for fo in range(8):
            P.add(("down", l, fo), _kc_major(wd[:, fo * 128:(fo + 1) * 128]))
    g2, tot = group_layout()
    assert tot == P.off and all(g2[k] == P.groups[k] for k in g2)
    return np.concatenate(P.parts, axis=1)


def vec_layout():
    off = 0
    v = {}

    def add(name, n):
        nonlocal off
        v[name] = (off, n)
        off += n
    add("c", 8)
    add("gfin", 8)
    for l in range(L_):
        add(("adab", l), 48)
        add(("gmix", l), 8)
        add(("gffn", l), 8)
        add(("mq", l), 2)
        add(("mkv", l), 1)
        add(("gq", l), 1)
        add(("gqs", l), 1)
        add(("gk", l), 1)
        add(("gks", l), 1)
    return v, off


def pack_vec(inp, b):
    v, n = vec_layout()
    out = np.zeros((128, n), np.float32)

    def put(name, arr):
        o, m = v[name]
        out[:, o:o + m] = arr
    put("c", inp["c"][b].reshape(8, 128).T)
    put("gfin", inp["final_norm_g"].reshape(8, 128).T)
    sw_g = _swap_idx(1, 64, [(0, 32), (32, 32)])
    for l in range(L_):
        put(("adab", l), inp["ada_b"][l].reshape(48, 128).T)
        put(("gmix", l), inp["norm_mix_g"][l].reshape(8, 128).T)
        put(("gffn", l), inp["norm_ffn_g"][l].reshape(8, 128).T)
        put(("mq", l), inp["mla_q_norm_g"][l].reshape(2, 128).T)
        put(("mkv", l), inp["mla_kv_norm_g"][l].reshape(1, 128).T)
        gq = inp["gqa_q_norm_g"][l]; gk = inp["gqa_k_norm_g"][l]
        put(("gq", l), np.concatenate([gq, gq]).reshape(128, 1))
        put(("gqs", l), np.concatenate([gq[sw_g], gq[sw_g]]).reshape(128, 1))
        put(("gk", l), np.concatenate([gk, gk]).reshape(128, 1))
        put(("gks", l), np.concatenate([gk[sw_g], gk[sw_g]]).reshape(128, 1))
    return out


def rope_tables():
    t = np.arange(S_, dtype=np.int32)
    pos_t = t.astype(np.float32)
    pos_r = (t // 64).astype(np.float32)
    pos_c = (t % 64).astype(np.float32)
    inv = (1.0 / (np.float32(10000.0) ** (np.arange(16, dtype=np.float32) / np.float32(16)))).astype(np.float32)

    def cs(pos):
        ang = (pos[None, :] * inv[:, None]).astype(np.float32)
        c = np.cos(ang).astype(np.float32); s = np.sin(ang).astype(np.float32)
        return np.concatenate([c, c], 0), np.concatenate([-s, s], 0)
    tab = np.zeros((4, 128, S_), np.float32)
    cM, sM = cs(pos_t)
    tab[0, 64:96] = cM; tab[1, 64:96] = sM
    cR, sR = cs(pos_r); cC, sC = cs(pos_c)
    cG = np.concatenate([cR, cC], 0); sG = np.concatenate([sR, sC], 0)
    tab[2, 0:64] = cG; tab[2, 64:128] = cG
    tab[3, 0:64] = sG; tab[3, 64:128] = sG
    return tab


NA_PAT = [0, 1] + [2] * 12 + [3, 4]
NA_PAT_M = [0, 1, 2, 14, 15]


def na_kb(m):
    return int(np.clip(2 * m - 4, 0, 22))


def na_bias_tables(rpb):
    out = np.full((L_, 5, 128, 4, 5, 128), NEG, np.float32)
    p = np.arange(128)
    q = np.arange(128)
    for pi, m in enumerate(NA_PAT_M):
        kb = na_kb(m)
        r = 2 * m + q // 64
        qc = q % 64
        rs = np.clip(r - 4, 0, 24)
        ws = np.clip(qc - 8, 0, 48)
        for j in range(5):
            kr = kb + 2 * j + p // 64
            kcol = p % 64
            vr = (kr[:, None] >= rs[None, :]) & (kr[:, None] < rs[None, :] + 8)
            vc = (kcol[:, None] >= ws[None, :]) & (kcol[:, None] < ws[None, :] + 16)
            valid = vr & vc
            ro = np.clip(kr[:, None] - r[None, :] + 7, 0, 14)
            co = np.clip(kcol[:, None] - qc[None, :], -15, 15) + 15
            for l in range(L_):
                for h in range(4):
                    vals = rpb[l, h][ro, co]
                    out[l, pi, :, h, j, :] = np.where(valid, vals, np.float32(NEG))
    return out.reshape(L_, 5, 128, 4 * 640)


def build_program(dbg=None):
    nc = bass.Bass("TRN2", target_bir_lowering=False)
    G, WTOT = group_layout()
    V, NV = vec_layout()
    xin = nc.dram_tensor("xT", [D_, S_], F32, kind="ExternalInput").ap()
    wpk = nc.dram_tensor("wpk", [128, WTOT], F32, kind="ExternalInput").ap()
    vecd = nc.dram_tensor("vec", [128, NV], F32, kind="ExternalInput").ap()
    tabd = nc.dram_tensor("tab", [4, 128, S_], F32, kind="ExternalInput").ap()
    nabd = nc.dram_tensor("nab", [L_ * 5, 128, 2560], F32, kind="ExternalInput").ap()
    identd = nc.dram_tensor("ident", [128, 128], F32, kind="ExternalInput").ap()
    yout = nc.dram_tensor("yT", [D_, S_], F32, kind="ExternalOutput").ap()
    dbgout = None
    if dbg is not None:
        dbgout = nc.dram_tensor("dbg", [128, 16384], F32, kind="ExternalOutput").ap()

    S = Sched(nc)
    R = S.R
    with contextlib.ExitStack() as es:
        def sbt(name, shape, dt):
            return es.enter_context(nc.sbuf_tensor(name, shape, dt))
        xT = sbt("xT_sb", [128, KC * S_], F32)
        hT = sbt("hT_sb", [128, KC * S_], BF16)
        UO = sbt("UO_sb", [128, 34816], BF16)
        Wst = [sbt("W0_sb", [128, 4096], BF16), sbt("W1_sb", [128, 4096], BF16)]
        PT = [sbt("PT%d_sb" % i, [128, 1024], BF16) for i in range(3)]
        TF = [sbt("TF%d_sb" % i, [128, 512], F32) for i in range(4)]
        TB = [sbt("TB%d_sb" % i, [128, 512], BF16) for i in range(2)]
        TABc = [sbt("TABc%d_sb" % i, [128, 512], F32) for i in range(2)]
        TABs = [sbt("TABs%d_sb" % i, [128, 512], F32) for i in range(2)]
        IDENT = sbt("IDENT_sb", [128, 128], BF16)
        VEC = sbt("VEC_sb", [128, NV], F32)
        MODS = [sbt("MOD%d_sb" % i, [128, 64], F32) for i in range(L_)]
        cur = {"l": 0}
        SCB = sbt("SCB_sb", [128, 8], BF16)
        SMALL = sbt("SMALL_sb", [128, 16], F32)
        ONES = sbt("ONES_sb", [128, 128], BF16)
        BONES = sbt("BONES_sb", [128, 128], BF16)
        EPSC = sbt("EPS_sb", [128, 1], F32)
        PS = es.enter_context(nc.psum_tensor("PS", [128, 4096], F32))

        OT0 = 0
        U0 = 16384

        def oT(c, t0, t1, rows=slice(0, 128)):
            return UO[rows, OT0 + c * S_ + t0: OT0 + c * S_ + t1]

        def U(off, n, rows=slice(0, 128)):
            return UO[rows, U0 + off: U0 + off + n]

        def bank(i):
            return PS[:, i * 512:(i + 1) * 512]

        state = {"bank": 0, "pair": 0, "pt": 0, "tf": 0, "tb": 0, "w": 0, "tab": 0, "evac": 0}

        def nbank():
            b = state["bank"]; state["bank"] = (b + 1) % 8
            return b

        def ntf():
            i = state["tf"]; state["tf"] = (i + 1) % 4
            return i

        def ntb():
            i = state["tb"]; state["tb"] = (i + 1) % 2
            return i

        dsem_w = [S.new_dma_sem(), S.new_dma_sem()]
        dsem_x = S.new_dma_sem()
        dsem_vec = S.new_dma_sem()
        dsem_tab = [S.new_dma_sem(), S.new_dma_sem()]
        dsem_bias = S.new_dma_sem()
        dsem_out = [S.new_dma_sem() for _ in range(4)]
        dsem_id = S.new_dma_sem()

        def loadw(gname):
            off, n = G[gname]
            i = state["w"]; state["w"] = 1 - i
            S.dma("pool", lambda e, i=i, off=off, n=n: e.dma_start(out=Wst[i][:, 0:n], in_=wpk[:, off:off + n]),
                  writes=[R("W", i)], sem=dsem_w[i])
            return i

        steps = []

        extra_q = []

        def step(g, fn, popx=False):
            steps.append((g, fn))
            if (g is not None or popx) and extra_q and not state.get("in_extra"):
                state["in_extra"] = True
                extra_q.pop(0)()
                state["in_extra"] = False

        def flush_extras():
            while extra_q:
                state["in_extra"] = True
                extra_q.pop(0)()
                state["in_extra"] = False

        def run_steps():
            nxt = {}
            widx_for = {}
            order = [i for i, (g, _) in enumerate(steps) if g is not None]
            pos = 0
            if order:
                widx_for[order[0]] = loadw(steps[order[0]][0])
            for i, (g, fn) in enumerate(steps):
                if g is not None:
                    cur = widx_for[i]
                    pos += 1
                    if pos < len(order):
                        widx_for[order[pos]] = loadw(steps[order[pos]][0])
                    fn(cur)
                else:
                    fn(None)
            steps.clear()

        def Wap(i, off, n, rows=slice(0, 128)):
            return Wst[i][rows, off:off + n]

        def mm(out, lhsT, rhs, start, stop, reads, writes):
            S.op("pe", lambda e: e.matmul(out, lhsT, rhs, start=start, stop=stop), reads=reads, writes=writes)

        def evac_copy(out, in_, reads, writes):
            k = state["evac"]; state["evac"] = 1 - k
            if k == 0:
                S.op("dve", lambda e: e.tensor_copy(out=out, in_=in_), reads=reads, writes=writes)
            else:
                S.op("act", lambda e: e.activation(out=out, in_=in_, func=AF.Copy), reads=reads, writes=writes)

        def vcol(name, j=0, n=1):
            o, m = V[name]
            return VEC[:, o + j:o + j + n]

        S.dma("sp", lambda e: e.dma_start(out=VEC[:], in_=vecd), writes=[R("VEC")], sem=dsem_vec)
        for kc in range(KC):
            S.dma("sp", lambda e, kc=kc: e.dma_start(out=xT[:, kc * S_:(kc + 1) * S_], in_=xin[kc * 128:(kc + 1) * 128, :]),
                  writes=[R("x", kc, tc) for tc in range(NTC)], sem=dsem_x)
        S.dma("pool", lambda e: e.dma_start(out=IDENT[:], in_=identd), writes=[R("IDENT")], sem=dsem_id)
        S.op("dve", lambda e: e.memset(ONES[:], 1.0), writes=[R("ONES")])
        S.op("dve", lambda e: e.memset(BONES[:], 0.0), writes=[R("BONES")])
        S.op("dve", lambda e: e.memset(BONES[0:64, 0:64], 1.0), writes=[R("BONES")])
        S.op("dve", lambda e: e.memset(BONES[64:128, 64:128], 1.0), writes=[R("BONES")])
        S.op("dve", lambda e: e.memset(EPSC[:], EPS), writes=[R("EPS")])
        S.op("act", lambda e: e.activation(out=SMALL[:, 0:8], in_=vcol("c", 0, 8), func=AF.Tanh, scale=0.5),
             reads=[R("VEC")], writes=[R("SMALL")])
        S.op("dve", lambda e: e.scalar_tensor_tensor(out=SMALL[:, 8:16], in0=SMALL[:, 0:8], scalar=1.0, in1=vcol("c", 0, 8),
                                                     op0=ALU.add, op1=ALU.mult), reads=[R("SMALL"), R("VEC")], writes=[R("SMALL2")])
        S.op("dve", lambda e: e.tensor_scalar(out=SCB[:], in0=SMALL[:, 8:16], scalar1=0.5, scalar2=None, op0=ALU.mult),
             reads=[R("SMALL2")], writes=[R("SCB")])

        def ada_step(l, i):
            def f(w):
                b = state["ada_bank"]() if state.get("ada_bank") else nbank()
                for fo4 in range(4):
                    for kc in range(KC):
                        mm(PS[:, b * 512 + fo4: b * 512 + fo4 + 1], Wap(w, kc * 512 + fo4 * 128, 128), SCB[:, kc:kc + 1],
                           kc == 0, kc == KC - 1, [R("W", w), R("SCB")], [R("ps", b)])
                col = (i // 2) * 8 + (i % 2) * 4
                if state.get("ada_bank"):
                    for c4 in range(4):
                        S.op("act", lambda e: e.activation(out=MODS[l][:, col + c4:col + c4 + 1], in_=PS[:, b * 512 + c4:b * 512 + c4 + 1],
                                                           func=AF.Identity, bias=vcol(("adab", l), col + c4), scale=1.0),
                             reads=[R("ps", b), R("VEC")], writes=[R("MOD", l)])
                else:
                    S.op("dve", lambda e: e.tensor_tensor(
                        out=MODS[l][:, col:col + 4], in0=PS[:, b * 512:b * 512 + 4], in1=vcol(("adab", l), col, 4), op=ALU.add),
                        reads=[R("ps", b), R("VEC")], writes=[R("MOD", l)])
            step(("ada", l, i), f)

        def ada_fin_m(l):
            def g(_):
                S.op("dve", lambda e: e.scalar_tensor_tensor(out=MODS[l][:, 48:56], in0=MODS[l][:, 8:16], scalar=1.0, in1=vcol(("gmix", l), 0, 8),
                                                             op0=ALU.add, op1=ALU.mult), reads=[R("MOD", l), R("VEC")], writes=[R("MOD", l)])
            step(None, g)

        def ada_fin_rest(l):
            def g(_):
                S.op("dve", lambda e: e.scalar_tensor_tensor(out=MODS[l][:, 56:64], in0=MODS[l][:, 32:40], scalar=1.0, in1=vcol(("gffn", l), 0, 8),
                                                             op0=ALU.add, op1=ALU.mult), reads=[R("MOD", l), R("VEC")], writes=[R("MOD", l)])
                S.op("dve", lambda e: e.tensor_scalar(out=MODS[l][:, 16:24], in0=MODS[l][:, 16:24], scalar1=0.5, scalar2=None, op0=ALU.mult),
                     reads=[R("MOD", l)], writes=[R("MOD", l)])
                S.op("dve", lambda e: e.tensor_scalar(out=MODS[l][:, 40:48], in0=MODS[l][:, 40:48], scalar1=0.5, scalar2=None, op0=ALU.mult),
                     reads=[R("MOD", l)], writes=[R("MOD", l)])
            step(None, g)

        def rstd_from_bank(b, nfeat, tfi, rows=slice(0, 128), n=512):
            S.op("act", lambda e: e.activation(out=TF[tfi][rows, 0:n], in_=PS[rows, b * 512:b * 512 + n], func=AF.Sqrt,
                                               bias=EPSC[rows, 0:1], scale=1.0 / nfeat),
                 reads=[R("ps", b), R("EPS")], writes=[R("TF", tfi)])
            S.op("dve", lambda e: e.reciprocal(out=TF[tfi][rows, 0:n], in_=TF[tfi][rows, 0:n]),
                 reads=[R("TF", tfi)], writes=[R("TF", tfi)])

        def norm_tc(l, gcol, shcol, tc):
            def f(_):
                M_ = MODS[l]
                t0 = tc * 512
                b = nbank()
                for kc in range(KC):
                    ti = ntb()
                    S.op("act", lambda e: e.activation(out=TB[ti][:, 0:512], in_=xT[:, kc * S_ + t0: kc * S_ + t0 + 512], func=AF.Square),
                         reads=[R("x", kc, tc)], writes=[R("TB", ti)])
                    mm(bank(b), ONES[:, :], TB[ti][:, 0:512], kc == 0, kc == KC - 1, [R("ONES"), R("TB", ti)], [R("ps", b)])
                rt = ntf()
                rstd_from_bank(b, float(D_), rt)
                for kc in range(KC):
                    t2 = ntf()
                    if t2 == rt:
                        t2 = ntf()
                    S.op("dve", lambda e: e.scalar_tensor_tensor(
                        out=TF[t2][:, 0:512], in0=xT[:, kc * S_ + t0: kc * S_ + t0 + 512], scalar=M_[:, gcol + kc: gcol + kc + 1],
                        in1=TF[rt][:, 0:512], op0=ALU.mult, op1=ALU.mult),
                        reads=[R("x", kc, tc), R("MOD", l), R("TF", rt)], writes=[R("TF", t2)])
                    S.op("act", lambda e: e.activation(
                        out=hT[:, kc * S_ + t0: kc * S_ + t0 + 512], in_=TF[t2][:, 0:512], func=AF.Identity,
                        bias=M_[:, shcol + kc: shcol + kc + 1], scale=1.0),
                        reads=[R("TF", t2), R("MOD", l)], writes=[R("h", kc, tc)])
            step(None, f)

        def hsl(kc, t0, n):
            return hT[:, kc * S_ + t0: kc * S_ + t0 + n]

        def hres(tc):
            return [R("h", kc, tc) for kc in range(KC)]

        VOFF = 8192 + 4096

        def vaug_ones(voff, layoutB):
            cols = [(0, 64), (128, 64), (192, 64), (320, 64)] if layoutB else [(64, 64), (256, 64)]
            def f(_):
                for tt in range(16):
                    for (c0, n) in cols:
                        S.op("dve", lambda e, tt=tt, c0=c0, n=n: e.memset(U(voff + tt * 384 + c0, n), 1.0), writes=[R("V", tt)])
            step(None, f)

        def vaug(voff, tt, c0):
            return U(voff + tt * 384 + c0, 128)

        def attention(qT, kT, qres, kres, vfn, scale, odd, outfn, outres, before_tc=None, after_tc=None, npairs=3, depth=2):
            orow = slice(64, 128) if odd else slice(0, 64)
            srow = slice(0, 64) if odd else slice(64, 128)
            for tc in range(NTC):
                if before_tc is not None:
                    before_tc(tc)
                acc = 6 + (state["pair"] % 2)
                state["pair"] += 1
                pairs = list(range(npairs))

                def qk(jp):
                    pb = pairs[jp % npairs]
                    for jj in range(2):
                        j = 2 * jp + jj
                        b = 2 * pb + jj
                        mm(bank(b), kT(j), qT(tc), True, True, [kres(j), qres(tc)], [R("ps", b)])
                for d_ in range(depth):
                    qk(d_)
                for jp in range(8):
                    if jp + depth < 8:
                        qk(jp + depth)
                    pb = pairs[jp % npairs]
                    pi = state["pt"]; state["pt"] = (pi + 1) % 3
                    S.op("act", lambda e, pb=pb, pi=pi: e.activation(out=PT[pi][:, 0:1024], in_=PS[:, pb * 1024:(pb + 1) * 1024], func=AF.Exp, scale=scale),
                         reads=[R("ps", 2 * pb), R("ps", 2 * pb + 1)], writes=[R("PT", pi)])
                    for jj in range(2):
                        j = 2 * jp + jj
                        mm(bank(acc), vfn(j), PT[pi][:, jj * 512:(jj + 1) * 512], j == 0, j == 15, [R("V", j), R("PT", pi)], [R("ps", acc)])
                ti = ntf()
                S.op("act", lambda e, ti=ti, acc=acc: e.activation(out=TF[ti][orow, 0:512], in_=PS[srow, acc * 512:(acc + 1) * 512], func=AF.Copy),
                     reads=[R("ps", acc)], writes=[R("TF", ti)])
                S.op("dve", lambda e, ti=ti: e.reciprocal(out=TF[ti][orow, 0:512], in_=TF[ti][orow, 0:512]), reads=[R("TF", ti)], writes=[R("TF", ti)])
                S.op("dve", lambda e, ti=ti, acc=acc, tc=tc: e.tensor_tensor(out=outfn(tc, orow), in0=PS[orow, acc * 512:(acc + 1) * 512],
                                                                          in1=TF[ti][orow, 0:512], op=ALU.mult),
                     reads=[R("ps", acc), R("TF", ti)], writes=[outres(tc)])
                if after_tc is not None:
                    after_tc(tc)

        def na_branch(l):
            QO, KO, VO = 0, 4096, 8192

            def proj(w):
                for ti in range(4):
                    dst = QO if ti < 2 else KO
                    p = ti % 2
                    for tc in range(NTC):
                        b = nbank()
                        for kc in range(KC):
                            mm(bank(b), Wap(w, kc * 512 + ti * 128, 128), hsl(kc, tc * 512, 512), kc == 0, kc == KC - 1,
                               [R("W", w), R("h", kc, tc)], [R("ps", b)])
                        evac_copy(U(dst + p * S_ + tc * 512, 512), bank(b), [R("ps", b)], [R("QK", ti, tc)])
            step(("naqk", l), proj)
            vaug_ones(VO, False)

            def projv(w):
                for tt in range(16):
                    b = nbank()
                    for kc in range(KC):
                        mm(PS[:, b * 512: b * 512 + 256], hsl(kc, tt * 128, 128), Wap(w, kc * 256, 256), kc == 0, kc == KC - 1,
                           [R("W", w), R("h", kc, tt // 4)], [R("ps", b)])
                    for (s0, n, d0) in [(0, 64, 0), (64, 128, 128), (192, 64, 320)]:
                        evac_copy(U(VO + tt * 384 + d0, n), PS[:, b * 512 + s0: b * 512 + s0 + n], [R("ps", b)], [R("V", tt)])
            step(("nav", l), projv)

            def attn(_):
                items = [(m, h) for m in range(16) for h in range(4)]
                n = len(items)
                st = {}

                BOFF = 14336

                def stA(i):
                    m, h = items[i]
                    if h == 0 and (m == 0 or NA_PAT[m] != NA_PAT[m - 1]):
                        pat = NA_PAT[m]
                        S.dma("pool", lambda e: e.dma_start(out=U(BOFF, 2560), in_=nabd[l * 5 + pat]), writes=[R("BIAS")], sem=dsem_bias)
                        S.op("dve", lambda e: e.tensor_scalar(out=U(BOFF, 2560), in0=U(BOFF, 2560), scalar1=8.0, scalar2=None, op0=ALU.mult),
                             reads=[R("BIAS")], writes=[R("BIAS")])
                    kt0 = na_kb(m) // 2
                    p = h // 2
                    rows = slice((h % 2) * 64, (h % 2) * 64 + 64)
                    pb = i % 3
                    for j in range(5):
                        col = pb * 1024 + j * 128
                        mm(PS[:, col:col + 128], U(KO + p * S_ + (kt0 + j) * 128, 128, rows), U(QO + p * S_ + m * 128, 128, rows), True, False,
                           [R("QK", 2 + p, (kt0 + j) // 4), R("QK", p, m // 4)], [R("ps", 2 * pb + (j // 4))])
                        mm(PS[:, col:col + 128], IDENT[:, :], U(BOFF + h * 640 + j * 128, 128), False, True,
                           [R("IDENT"), R("BIAS")], [R("ps", 2 * pb + (j // 4))])

                def stB(i):
                    m, h = items[i]
                    kt0 = na_kb(m) // 2
                    pb = i % 3
                    pi = i % 3
                    S.op("act", lambda e: e.activation(out=PT[pi][:, 0:640], in_=PS[:, pb * 1024: pb * 1024 + 640], func=AF.Exp, scale=0.125),
                         reads=[R("ps", 2 * pb), R("ps", 2 * pb + 1)], writes=[R("PT", pi)])
                    acc = 6 + (i % 2)
                    c0 = [0, 64, 192, 256][h]
                    for j in range(5):
                        mm(PS[:, acc * 512: acc * 512 + 128], vaug(VO, kt0 + j, c0), PT[pi][:, j * 128:(j + 1) * 128], j == 0, j == 4,
                           [R("V", kt0 + j), R("PT", pi)], [R("ps", acc)])

                def stC(i):
                    m, h = items[i]
                    p = h // 2
                    odd = (h % 2) == 1
                    acc = 6 + (i % 2)
                    orow = slice(64, 128) if odd else slice(0, 64)
                    srow = slice(0, 64) if odd else slice(64, 128)
                    t2 = ntf()
                    S.op("act", lambda e: e.activation(
                        out=TF[t2][orow, 0:128], in_=PS[srow, acc * 512: acc * 512 + 128], func=AF.Copy),
                        reads=[R("ps", acc)], writes=[R("TF", t2)])
                    S.op("dve", lambda e: e.reciprocal(out=TF[t2][orow, 0:128], in_=TF[t2][orow, 0:128]),
                         reads=[R("TF", t2)], writes=[R("TF", t2)])
                    S.op("dve", lambda e: e.tensor_tensor(
                        out=oT(p, m * 128, m * 128 + 128, orow), in0=PS[orow, acc * 512: acc * 512 + 128], in1=TF[t2][orow, 0:128], op=ALU.mult),
                        reads=[R("ps", acc), R("TF", t2)], writes=[R("o", p, m // 4)])

                for i in range(n + 2):
                    if i < n:
                        stA(i)
                    if 0 <= i - 1 < n:
                        stB(i - 1)
                    if 0 <= i - 2 < n:
                        stC(i - 2)
            step(None, attn)

        def load_tab(ci, si, tc, rows):
            k = state["tab"]; state["tab"] = 1 - k
            S.dma("sp", lambda e, k=k: e.dma_start(out=TABc[k][rows, :], in_=tabd[ci, rows, tc * 512:(tc + 1) * 512]),
                  writes=[R("TABc", k)], sem=dsem_tab[k])
            S.dma("sp", lambda e, k=k: e.dma_start(out=TABs[k][rows, :], in_=tabd[si, rows, tc * 512:(tc + 1) * 512]),
                  writes=[R("TABs", k)], sem=dsem_tab[k])
            return k

        def rope_combine(bA, bB, rows, k, gA, gB, out_ap, out_res, rstd_tf=None):
            t1 = ntf()
            while rstd_tf is not None and t1 == rstd_tf:
                t1 = ntf()
            t2 = ntf()
            while (rstd_tf is not None and t2 == rstd_tf) or t2 == t1:
                t2 = ntf()
            if gA is None:
                S.op("dve", lambda e: e.tensor_tensor(out=TF[t1][rows, 0:512], in0=PS[rows, bA * 512:(bA + 1) * 512], in1=TABc[k][rows, :], op=ALU.mult),
                     reads=[R("ps", bA), R("TABc", k)], writes=[R("TF", t1)])
                S.op("dve", lambda e: e.tensor_tensor(out=TF[t2][rows, 0:512], in0=PS[rows, bB * 512:(bB + 1) * 512], in1=TABs[k][rows, :], op=ALU.mult),
                     reads=[R("ps", bB), R("TABs", k)], writes=[R("TF", t2)])
            else:
                S.op("dve", lambda e: e.scalar_tensor_tensor(out=TF[t1][rows, 0:512], in0=PS[rows, bA * 512:(bA + 1) * 512], scalar=gA,
                                                             in1=TABc[k][rows, :], op0=ALU.mult, op1=ALU.mult),
                     reads=[R("ps", bA), R("TABc", k), R("VEC")], writes=[R("TF", t1)])
                S.op("dve", lambda e: e.scalar_tensor_tensor(out=TF[t2][rows, 0:512], in0=PS[rows, bB * 512:(bB + 1) * 512], scalar=gB,
                                                             in1=TABs[k][rows, :], op0=ALU.mult, op1=ALU.mult),
                     reads=[R("ps", bB), R("TABs", k), R("VEC")], writes=[R("TF", t2)])
            if rstd_tf is None:
                S.op("dve", lambda e: e.tensor_tensor(out=out_ap, in0=TF[t1][rows, 0:512], in1=TF[t2][rows, 0:512], op=ALU.add),
                     reads=[R("TF", t1), R("TF", t2)], writes=[out_res])
            else:
                S.op("pool", lambda e: e.tensor_tensor(out=TF[t1][rows, 0:512], in0=TF[t1][rows, 0:512], in1=TF[t2][rows, 0:512], op=ALU.add),
                     reads=[R("TF", t1), R("TF", t2)], writes=[R("TF", t1)])
                S.op("pool", lambda e: e.tensor_tensor(out=out_ap, in0=TF[t1][rows, 0:512], in1=TF[rstd_tf][rows, 0:512], op=ALU.mult),
                     reads=[R("TF", t1), R("TF", rstd_tf)], writes=[out_res])

        def mla_branch(l):
            CQ, CKV, KPE, VO, QH, KH = 0, 4096, 6144, 8192, 14336, 16384

            def projA(w):
                for tc in range(NTC):
                    bs = []
                    for ti in range(3):
                        b = nbank(); bs.append(b)
                        for kc in range(KC):
                            mm(bank(b), Wap(w, kc * 384 + ti * 128, 128), hsl(kc, tc * 512, 512), kc == 0, kc == KC - 1,
                               [R("W", w), R("h", kc, tc)], [R("ps", b)])
                    for (tis, nfeat, vname, dst) in [((0, 1), 256.0, ("mq", l), CQ), ((2,), 128.0, ("mkv", l), CKV)]:
                        bS = nbank()
                        for n_, ti in enumerate(tis):
                            tb = ntb()
                            S.op("act", lambda e, tb=tb, b=bs[ti]: e.activation(out=TB[tb][:, 0:512], in_=bank(b), func=AF.Square),
                                 reads=[R("ps", bs[ti])], writes=[R("TB", tb)])
                            mm(bank(bS), ONES[:, :], TB[tb][:, 0:512], n_ == 0, n_ == len(tis) - 1, [R("ONES"), R("TB", tb)], [R("ps", bS)])
                        rt = ntf()
                        rstd_from_bank(bS, nfeat, rt)
                        for n_, ti in enumerate(tis):
                            S.op("dve", lambda e, n_=n_, b=bs[ti], rt=rt, dst=dst, vname=vname: e.scalar_tensor_tensor(
                                out=U(dst + n_ * S_ + tc * 512, 512), in0=bank(b), scalar=vcol(vname, n_), in1=TF[rt][:, 0:512],
                                op0=ALU.mult, op1=ALU.mult), reads=[R("ps", bs[ti]), R("TF", rt), R("VEC")], writes=[R("C", dst, n_, tc)])
            step(("mlaA", l), projA)

            def projB(w):
                rows = slice(64, 96)
                knext = load_tab(0, 1, 0, rows)
                for tc in range(NTC):
                    k = knext
                    if tc + 1 < NTC:
                        knext = load_tab(0, 1, tc + 1, rows)
                    bA = nbank(); bB = nbank()
                    for (b, c0) in [(bA, 0), (bB, 96)]:
                        for kc in range(KC):
                            mm(PS[0:96, b * 512:(b + 1) * 512], Wap(w, kc * 192 + c0, 96), hsl(kc, tc * 512, 512), kc == 0, kc == KC - 1,
                               [R("W", w), R("h", kc, tc)], [R("ps", b)])
                    rope_combine(bA, bB, rows, k, None, None, U(KPE + tc * 512, 512, rows), R("KPE", tc))
            step(("mlaB", l), projB)
            vaug_ones(VO, False)

            def up(w):
                UQ, KN, VV = 0, 1536, 1792
                for tt in range(16):
                    b = nbank()
                    mm(PS[:, b * 512: b * 512 + 256], U(CKV + tt * 128, 128), Wap(w, VV, 256), True, True,
                       [R("W", w), R("C", CKV, 0, tt // 4)], [R("ps", b)])
                    for (s0, n, d0) in [(0, 64, 0), (64, 128, 128), (192, 64, 320)]:
                        evac_copy(U(VO + tt * 384 + d0, n), PS[:, b * 512 + s0: b * 512 + s0 + n], [R("ps", b)], [R("V", tt)])
                def emit_K(h):
                    for tc in range(NTC):
                        b = nbank()
                        mm(PS[0:64, b * 512:(b + 1) * 512], Wap(w, KN + h * 64, 64), U(CKV + tc * 512, 512), True, True,
                           [R("W", w), R("C", CKV, 0, tc)], [R("ps", b)])
                        evac_copy(U(KH + tc * 512, 512, slice(0, 64)), PS[0:64, b * 512:(b + 1) * 512], [R("ps", b)], [R("KH", tc)])
                        S.op("dve", lambda e: e.tensor_copy(out=U(KH + tc * 512, 512, slice(64, 96)), in_=U(KPE + tc * 512, 512, slice(64, 96))),
                             reads=[R("KPE", tc)], writes=[R("KH", tc)])

                def emit_Q(h, tc, k, banks=None):
                    if banks is None:
                        bA = nbank(); bB = nbank()
                    else:
                        bA, bB = banks
                    for (b2, c0) in [(bA, h * 192), (bB, h * 192 + 96)]:
                        for c in range(2):
                            mm(PS[0:96, b2 * 512:(b2 + 1) * 512], Wap(w, UQ + c * 768 + c0, 96), U(CQ + c * S_ + tc * 512, 512), c == 0, c == 1,
                               [R("W", w), R("C", CQ, c, tc)], [R("ps", b2)])
                    evac_copy(U(QH + tc * 512, 512, slice(0, 64)), PS[0:64, bA * 512:(bA + 1) * 512], [R("ps", bA)], [R("QH", tc)])
                    rope_combine(bA, bB, slice(64, 96), k, None, None, U(QH + tc * 512, 512, slice(64, 96)), R("QH", tc))

                for tc in range(NTC):
                    S.op("dve", lambda e: e.memset(U(QH + tc * 512, 512, slice(64, 128)), 0.0), writes=[R("QH", tc)])
                    S.op("dve", lambda e: e.memset(U(KH + tc * 512, 512, slice(64, 128)), 0.0), writes=[R("KH", tc)])
                knext = load_tab(0, 1, 0, slice(64, 96))
                for tc in range(NTC):
                    k = knext
                    if tc + 1 < NTC:
                        knext = load_tab(0, 1, tc + 1, slice(64, 96))
                    emit_Q(0, tc, k)
                for h in range(4):
                    emit_K(h)
                    odd = (h % 2) == 1
                    c0 = [0, 64, 192, 256][h]
                    tabk = {}
                    if h + 1 < 4:
                        def before_tc(tc, tabk=tabk):
                            tabk[tc] = load_tab(0, 1, tc, slice(64, 96))

                        def after_tc(tc, tabk=tabk, h=h):
                            emit_Q(h + 1, tc, tabk[tc], banks=(4, 5))
                    else:
                        before_tc = None
                        after_tc = None
                    attention(lambda tc: U(QH + tc * 512, 512, slice(0, 128)),
                              lambda j: U(KH + j * 128, 128, slice(0, 128)),
                              lambda tc: R("QH", tc), lambda j: R("KH", j // 4),
                              lambda j, c0=c0: vaug(VO, j, c0), float(96 ** -0.5), odd,
                              lambda tc, orow, h=h: oT(2 + h // 2, tc * 512, tc * 512 + 512, orow),
                              lambda tc, h=h: R("o", 2 + h // 2, tc), before_tc=before_tc, after_tc=after_tc, npairs=2, depth=1)
            step(("mlaup", l), up)

        def gqa_branch(l, before_attn=None, after_attn=None):
            QO, KO, VO = 0, 8192, 12288

            def qk_group(w, tiles, gname, gsname):
                knext = load_tab(2, 3, 0, slice(0, 128))
                for tc in range(NTC):
                    k = knext
                    if tc + 1 < NTC:
                        knext = load_tab(2, 3, tc + 1, slice(0, 128))
                    for (wc0, dst_off, resname, ridx) in tiles:
                        bA = nbank(); bB = nbank()
                        for (b, c0) in [(bA, wc0), (bB, wc0 + 256)]:
                            for kc in range(KC):
                                mm(bank(b), Wap(w, kc * 512 + c0, 128), hsl(kc, tc * 512, 512), kc == 0, kc == KC - 1,
                                   [R("W", w), R("h", kc, tc)], [R("ps", b)])
                        tb = ntb()
                        S.op("act", lambda e: e.activation(out=TB[tb][:, 0:512], in_=bank(bA), func=AF.Square),
                             reads=[R("ps", bA)], writes=[R("TB", tb)])
                        bS = nbank()
                        mm(bank(bS), BONES[:, :], TB[tb][:, 0:512], True, True, [R("BONES"), R("TB", tb)], [R("ps", bS)])
                        rt = ntf()
                        rstd_from_bank(bS, 64.0, rt)
                        rope_combine(bA, bB, slice(0, 128), k, vcol((gname, l)), vcol((gsname, l)),
                                     U(dst_off + tc * 512, 512), R(resname, ridx, tc), rstd_tf=rt)

            step(("gqa", l), lambda w: qk_group(w, [(p * 128, QO + p * S_, "GQ", p) for p in range(2)], "gq", "gqs"))
            step(("gqb", l), lambda w: qk_group(w, [(p * 128, QO + (2 + p) * S_, "GQ", 2 + p) for p in range(2)], "gq", "gqs"))
            step(("gk", l), lambda w: qk_group(w, [(g * 128, KO + g * S_, "GK", g) for g in range(2)], "gk", "gks"))
            vaug_ones(VO, True)

            def pv(w):
                for tt in range(16):
                    b = nbank()
                    for kc in range(KC):
                        mm(PS[:, b * 512: b * 512 + 128], hsl(kc, tt * 128, 128), Wap(w, kc * 128, 128), kc == 0, kc == KC - 1,
                           [R("W", w), R("h", kc, tt // 4)], [R("ps", b)])
                    for (s0, d0) in [(0, 64), (64, 256)]:
                        evac_copy(U(VO + tt * 384 + d0, 64), PS[:, b * 512 + s0: b * 512 + s0 + 64], [R("ps", b)], [R("V", tt)])
            step(("gv", l), pv)

            def attn_pair_tc(p, tc):
                def f(_):
                    g = p // 2
                    vA = g * 192 + 64
                    vB = g * 192
                    accA = 4 + 2 * (state["pair"] % 2)
                    accB = accA + 1
                    state["pair"] += 1

                    def qk(j):
                        pb = j % 2
                        for hh in range(2):
                            rows = slice(hh * 64, hh * 64 + 64)
                            mm(bank(2 * pb + hh), U(KO + g * S_ + j * 128, 128, rows), U(QO + p * S_ + tc * 512, 512, rows), True, True,
                               [R("GK", g, j // 4), R("GQ", p, tc)], [R("ps", 2 * pb + hh)])
                    qk(0)
                    for j in range(16):
                        if j + 1 < 16:
                            qk(j + 1)
                        pb = j % 2
                        pi = state["pt"]; state["pt"] = (pi + 1) % 3
                        S.op("act", lambda e: e.activation(out=PT[pi][:, 0:1024], in_=PS[:, pb * 1024:(pb + 1) * 1024], func=AF.Exp, scale=0.125),
                             reads=[R("ps", 2 * pb), R("ps", 2 * pb + 1)], writes=[R("PT", pi)])
                        mm(bank(accA), vaug(VO, j, vA), PT[pi][:, 0:512], j == 0, j == 15, [R("V", j), R("PT", pi)], [R("ps", accA)])
                        mm(bank(accB), vaug(VO, j, vB), PT[pi][:, 512:1024], j == 0, j == 15, [R("V", j), R("PT", pi)], [R("ps", accB)])
                    ti = ntf()
                    for (acc, orow, srow) in [(accA, slice(0, 64), slice(64, 128)), (accB, slice(64, 128), slice(0, 64))]:
                        S.op("act", lambda e: e.activation(out=TF[ti][orow, 0:512], in_=PS[srow, acc * 512:(acc + 1) * 512], func=AF.Copy),
                             reads=[R("ps", acc)], writes=[R("TF", ti)])
                    S.op("dve", lambda e: e.reciprocal(out=TF[ti][:, 0:512], in_=TF[ti][:, 0:512]), reads=[R("TF", ti)], writes=[R("TF", ti)])
                    for (acc, orow, srow) in [(accA, slice(0, 64), slice(64, 128)), (accB, slice(64, 128), slice(0, 64))]:
                        S.op("dve", lambda e: e.tensor_tensor(out=oT(4 + p, tc * 512, tc * 512 + 512, orow), in0=PS[orow, acc * 512:(acc + 1) * 512],
                                                              in1=TF[ti][orow, 0:512], op=ALU.mult),
                             reads=[R("ps", acc), R("TF", ti)], writes=[R("o", 4 + p, tc)])
                step(None, f, popx=True)
            if before_attn is not None:
                before_attn()
            step(None, lambda _: state.__setitem__("ada_bank", lambda: 4 + 2 * (state["pair"] % 2)))
            for p in range(4):
                for tc in range(NTC):
                    attn_pair_tc(p, tc)
            step(None, lambda _: state.__setitem__("ada_bank", None))
            if after_attn is not None:
                after_attn()

        def merge(l):
            MO = 0
            brk = [(0, 2), (2, 2), (4, 4)]
            for fo in range(8):
                def f(w, fo=fo):
                    for tc in range(NTC):
                        acc = ntf()
                        boff = 3072
                        for br in range(3):
                            bG = nbank(); bY = nbank()
                            for kc in range(KC):
                                mm(bank(bG), Wap(w, br * 1024 + kc * 128, 128), hsl(kc, tc * 512, 512), kc == 0, kc == KC - 1,
                                   [R("W", w), R("h", kc, tc)], [R("ps", bG)])
                            c0, ncb = brk[br]
                            for c in range(ncb):
                                mm(bank(bY), Wap(w, boff + c * 128, 128), oT(c0 + c, tc * 512, tc * 512 + 512), c == 0, c == ncb - 1,
                                   [R("W", w), R("o", c0 + c, tc)], [R("ps", bY)])
                            boff += ncb * 128
                            th = ntf()
                            while th == acc:
                                th = ntf()
                            S.op("act", lambda e, th=th, bG=bG: e.activation(out=TF[th][:, 0:512], in_=bank(bG), func=AF.Tanh, scale=0.5),
                                 reads=[R("ps", bG)], writes=[R("TF", th)])
                            if br == 0:
                                S.op("dve", lambda e, th=th, bY=bY, acc=acc: e.scalar_tensor_tensor(
                                    out=TF[acc][:, 0:512], in0=TF[th][:, 0:512], scalar=1.0, in1=bank(bY), op0=ALU.add, op1=ALU.mult),
                                    reads=[R("TF", th), R("ps", bY)], writes=[R("TF", acc)])
                            else:
                                S.op("dve", lambda e, th=th, bY=bY: e.scalar_tensor_tensor(
                                    out=TF[th][:, 0:512], in0=TF[th][:, 0:512], scalar=1.0, in1=bank(bY), op0=ALU.add, op1=ALU.mult),
                                    reads=[R("TF", th), R("ps", bY)], writes=[R("TF", th)])
                                if br == 1:
                                    S.op("dve", lambda e, th=th, acc=acc: e.tensor_tensor(out=TF[acc][:, 0:512], in0=TF[acc][:, 0:512], in1=TF[th][:, 0:512], op=ALU.add),
                                         reads=[R("TF", th), R("TF", acc)], writes=[R("TF", acc)])
                                else:
                                    S.op("dve", lambda e, th=th, acc=acc, tc=tc: e.tensor_tensor(
                                        out=U(MO + fo * S_ + tc * 512, 512), in0=TF[acc][:, 0:512], in1=TF[th][:, 0:512], op=ALU.add),
                                        reads=[R("TF", th), R("TF", acc)], writes=[R("M", fo, tc)])
                step(("mrg", l, fo), f)
            for fo2 in range(8):
                def f2(w, fo2=fo2):
                    for tc in range(NTC):
                        b = nbank()
                        for fo in range(8):
                            mm(bank(b), Wap(w, fo * 128, 128), U(MO + fo * S_ + tc * 512, 512), fo == 0, fo == 7,
                               [R("W", w), R("M", fo, tc)], [R("ps", b)])
                        xs = xT[:, fo2 * S_ + tc * 512: fo2 * S_ + tc * 512 + 512]
                        S.op("dve", lambda e, b=b, xs=xs: e.scalar_tensor_tensor(out=xs, in0=bank(b), scalar=MODS[l][:, 16 + fo2: 17 + fo2], in1=xs,
                                                                             op0=ALU.mult, op1=ALU.add),
                             reads=[R("ps", b), R("MOD", l), R("x", fo2, tc)], writes=[R("x", fo2, tc)])
                step(("wout", l, fo2), f2)

        def ffn(l, hooks):
            def A(j, tc2):
                return UO[:, j * 1024 + tc2 * 512: j * 1024 + tc2 * 512 + 512]
            M_ = MODS[l]
            for half in range(2):
                for jj in range(11):
                    def f(w, jj=jj, half=half):
                        for j2 in range(2):
                            j = 2 * jj + j2
                            for tc2 in range(2):
                                tc = 2 * half + tc2
                                bG = nbank(); bU = nbank()
                                for (b, c0) in [(bG, j2 * 2048), (bU, j2 * 2048 + 1024)]:
                                    for kc in range(KC):
                                        mm(bank(b), Wap(w, c0 + kc * 128, 128), hsl(kc, tc * 512, 512), kc == 0, kc == KC - 1,
                                           [R("W", w), R("h", kc, tc)], [R("ps", b)])
                                th = ntf()
                                S.op("act", lambda e: e.activation(out=TF[th][:, 0:512], in_=bank(bG), func=AF.Tanh, scale=0.5),
                                     reads=[R("ps", bG)], writes=[R("TF", th)])
                                S.op("dve", lambda e: e.scalar_tensor_tensor(
                                    out=TF[th][:, 0:512], in0=TF[th][:, 0:512], scalar=1.0, in1=bank(bG), op0=ALU.add, op1=ALU.mult),
                                    reads=[R("TF", th), R("ps", bG)], writes=[R("TF", th)])
                                S.op("dve", lambda e: e.tensor_tensor(
                                    out=A(j, tc2), in0=TF[th][:, 0:512], in1=bank(bU), op=ALU.mult),
                                    reads=[R("TF", th), R("ps", bU)], writes=[R("A", j, tc2)])
                    step(("gu", l, jj), f)
                    for hk in hooks.get((half, "gu", jj), []):
                        hk()
                for fo in range(8):
                    def f2(w, fo=fo, half=half):
                        for tc2 in range(2):
                            tc = 2 * half + tc2
                            b = nbank()
                            for j in range(NJ):
                                mm(bank(b), Wap(w, j * 128, 128), A(j, tc2), j == 0, j == NJ - 1, [R("W", w), R("A", j, tc2)], [R("ps", b)])
                            xs = xT[:, fo * S_ + tc * 512: fo * S_ + tc * 512 + 512]
                            S.op("dve", lambda e: e.scalar_tensor_tensor(out=xs, in0=bank(b), scalar=M_[:, 40 + fo: 41 + fo], in1=xs,
                                                                         op0=ALU.mult, op1=ALU.add),
                                 reads=[R("ps", b), R("MOD", l), R("x", fo, tc)], writes=[R("x", fo, tc)])
                    step(("down", l, fo), f2)
                    for hk in hooks.get((half, "down", fo), []):
                        hk()

        def bar():
            step(None, lambda _: S.barrier())

        def final_tc(tc):
            def f(_):
                t0 = tc * 512
                b = nbank()
                for kc in range(KC):
                    ti = ntb()
                    S.op("act", lambda e: e.activation(out=TB[ti][:, 0:512], in_=xT[:, kc * S_ + t0: kc * S_ + t0 + 512], func=AF.Square),
                         reads=[R("x", kc, tc)], writes=[R("TB", ti)])
                    mm(bank(b), ONES[:, :], TB[ti][:, 0:512], kc == 0, kc == KC - 1, [R("ONES"), R("TB", ti)], [R("ps", b)])
                rt = ntf()
                rstd_from_bank(b, float(D_), rt)
                for kc in range(KC):
                    t2 = ntf()
                    if t2 == rt:
                        t2 = ntf()
                    S.op("dve", lambda e: e.scalar_tensor_tensor(
                        out=TF[t2][:, 0:512], in0=xT[:, kc * S_ + t0: kc * S_ + t0 + 512], scalar=vcol("gfin", kc),
                        in1=TF[rt][:, 0:512], op0=ALU.mult, op1=ALU.mult),
                        reads=[R("x", kc, tc), R("VEC"), R("TF", rt)], writes=[R("TF", t2)])
                    S.dma("sp", lambda e: e.dma_start(out=yout[kc * 128:(kc + 1) * 128, t0:t0 + 512], in_=TF[t2][:, 0:512]),
                          reads=[R("TF", t2)], writes=[], sem=dsem_out[t2])
            step(None, f)

        for i in range(4):
            ada_step(0, i)
        ada_fin_m(0)
        for tc in range(NTC):
            norm_tc(0, 48, 0, tc)
        for l in range(L_):
            if l == 0:
                extra_q.extend([(lambda i=i: ada_step(0, i)) for i in range(4, 12)])
            na_branch(l)
            bar()
            mla_branch(l)
            bar()
            last = (l + 1 == L_)

            def before_attn(l=l, last=last):
                flush_extras()
                if not last:
                    extra_q.extend([(lambda i=i: ada_step(l + 1, i)) for i in range(12)])

            def after_attn(l=l, last=last):
                flush_extras()
                if not last:
                    ada_fin_m(l + 1)
                    ada_fin_rest(l + 1)
            gqa_branch(l, before_attn, after_attn)
            bar()
            if l == 0:
                ada_fin_rest(0)
            merge(l)
            bar()
            norm_tc(l, 56, 24, 0)
            norm_tc(l, 56, 24, 1)

            def nxt(tc, l=l, last=last):
                if last:
                    final_tc(tc)
                else:
                    norm_tc(l + 1, 48, 0, tc)
            hooks = {
                (0, "gu", 2): [lambda l=l: norm_tc(l, 56, 24, 2)],
                (0, "gu", 5): [lambda l=l: norm_tc(l, 56, 24, 3)],
                (1, "gu", 2): [lambda: nxt(0)],
                (1, "gu", 5): [lambda: nxt(1)],
            }
            ffn(l, hooks)
            bar()
            nxt(2)
            nxt(3)
        run_steps()
        S.emit()
    return nc


_CACHE = {}


def kernel(**inputs):
    inp = {k: np.asarray(v) for k, v in inputs.items()}
    B = inp["x"].shape[0]
    wp = pack_weights(inp)
    tab = rope_tables()
    nab = na_bias_tables(inp["na_rpb"]).reshape(L_ * 5, 128, 2560)
    in_maps = []
    for b in range(B):
        in_maps.append({
            "xT": np.ascontiguousarray(inp["x"][b].T),
            "wpk": wp,
            "vec": pack_vec(inp, b),
            "tab": tab,
            "nab": nab,
            "ident": np.eye(128, dtype=np.float32),
        })
    nc = build_program()
    res = run_bass_kernel_spmd(nc, in_maps, core_ids=list(range(B)))
    out = np.stack([np.ascontiguousarray(r["yT"].T) for r in res.results], axis=0)
    return out.astype(np.float32)
```

```python
import contextlib
import numpy as np
import concourse.bass as bass
import concourse.mybir as mybir
from concourse.bass_utils import run_bass_kernel_spmd

F32 = mybir.dt.float32
BF16 = mybir.dt.bfloat16
AF = mybir.ActivationFunctionType
ALU = mybir.AluOpType

S_ = 2048
D_ = 1024
KC = 8
NTC = 4
L_ = 2
HID = 2816
NJ = 22
EPS = 1e-6
NEG = -30000.0

COMPUTE = ("pe", "act", "dve", "pool")


class Res:
    __slots__ = ("name", "last_w", "readers")

    def __init__(self, name):
        self.name = name
        self.last_w = None
        self.readers = []


class Ins:
    __slots__ = ("eng", "fn", "deps", "inc", "is_dma", "dsem", "dval", "tokval")

    def __init__(self, eng, fn):
        self.eng = eng
        self.fn = fn
        self.deps = []
        self.inc = False
        self.is_dma = False
        self.dsem = None
        self.dval = 0
        self.tokval = 0


class _Rec:
    def __init__(self):
        self.calls = []

    def __getattr__(self, name):
        def call(*a, **k):
            self.calls.append((name, a, k))
            return None
        return call


class Sched:
    def __init__(self, nc):
        self.nc = nc
        self.streams = {e: [] for e in ("pe", "act", "dve", "pool", "sp")}
        self.pending = {e: [] for e in self.streams}
        self.dma_sems = []
        self.resd = {}

    def R(self, *key):
        r = self.resd.get(key)
        if r is None:
            r = Res(key)
            self.resd[key] = r
        return r

    def new_dma_sem(self):
        rec = [None, 0, len(self.dma_sems)]
        self.dma_sems.append(rec)
        return rec

    def _add(self, eng, fn, reads, writes, is_dma=False, dsem=None):
        rec = _Rec()
        fn(rec)
        assert len(rec.calls) == 1
        name, a, k = rec.calls[0]
        fn = (lambda e, name=name, a=a, k=k: getattr(e, name)(*a, **k))
        ins = Ins(eng, fn)
        ins.is_dma = is_dma
        deps = []
        for r in reads:
            if r.last_w is not None:
                deps.append(r.last_w)
        for w in writes:
            if w.last_w is not None:
                deps.append(w.last_w)
            deps.extend(w.readers)
        deps.extend(self.pending[eng])
        self.pending[eng] = []
        seen = set()
        for d in deps:
            if id(d) in seen:
                continue
            seen.add(id(d))
            if (not d.is_dma) and d.eng == eng and eng in ("pe", "sp"):
                continue
            d.inc = True
            ins.deps.append((d, d.dsem[1] if d.is_dma else 0))
        if is_dma:
            ins.dsem = dsem
            dsem[1] += 16
            ins.dval = dsem[1]
            ins.inc = True
        for r in reads:
            r.readers.append(ins)
        for w in writes:
            w.last_w = ins
            w.readers = []
        self.streams[eng].append(ins)
        return ins

    def op(self, eng, fn, reads=(), writes=()):
        return self._add(eng, fn, list(reads), list(writes))

    def dma(self, queue, fn, reads=(), writes=(), sem=None):
        return self._add(queue, fn, list(reads), list(writes), is_dma=True, dsem=sem)

    def barrier(self):
        toks = []
        for e in self.streams:
            for ins in reversed(self.streams[e]):
                if not ins.is_dma:
                    if e != "sp":
                        toks.append(ins)
                    break
        lastd = {}
        for e in self.streams:
            for ins in self.streams[e]:
                if ins.is_dma:
                    lastd[ins.dsem[2]] = ins
        toks.extend(lastd.values())
        for t in toks:
            t.inc = True
        for e in self.streams:
            self.pending[e] = list(toks)

    def emit(self):
        nc = self.nc
        with contextlib.ExitStack() as es:
            esem = {}
            for e in COMPUTE:
                esem[e] = es.enter_context(nc.semaphore("s_" + e))
            for rec in self.dma_sems:
                rec[0] = es.enter_context(nc.semaphore("d%d" % rec[2]))
            for e in COMPUTE:
                nc.gpsimd.sem_clear(esem[e])
            for rec in self.dma_sems:
                nc.gpsimd.sem_clear(rec[0])
            for e in COMPUTE:
                k = 0
                for ins in self.streams[e]:
                    if ins.is_dma:
                        continue
                    if ins.inc:
                        k += 1
                        ins.tokval = k
            block = es.enter_context(nc.Block())
            streams = self.streams
            final_dma = [(rec[0], rec[1]) for rec in self.dma_sems if rec[1] > 0]

            def run(ename, engobj):
                known = {}
                for ins in streams[ename]:
                    for (d, need) in ins.deps:
                        if d.is_dma:
                            key = ("d", d.dsem[2]); sem = d.dsem[0]; val = need
                        else:
                            key = ("e", d.eng); sem = esem[d.eng]; val = d.tokval
                        if known.get(key, 0) >= val:
                            continue
                        known[key] = val
                        engobj.wait_ge(sem, val)
                    bi = ins.fn(engobj)
                    if ins.is_dma:
                        bi.then_inc(ins.dsem[0], 16)
                    elif ins.inc:
                        bi.then_inc(esem[ename], 1)
                if ename == "sp":
                    for sem, val in final_dma:
                        engobj.wait_ge(sem, val)

            @block.tensor
            def _(eng):
                run("pe", eng)

            @block.scalar
            def _(eng):
                run("act", eng)

            @block.vector
            def _(eng):
                run("dve", eng)

            @block.gpsimd
            def _(eng):
                run("pool", eng)

            @block.sync
            def _(eng):
                run("sp", eng)


def _kc_major(w, cols=None):
    K, M = w.shape
    nk = K // 128
    return np.ascontiguousarray(w.reshape(nk, 128, M).transpose(1, 0, 2).reshape(128, nk * M))


def _swap_idx(n_heads, dim, groups):
    idx = np.arange(n_heads * dim).reshape(n_heads, dim).copy()
    out = idx.copy()
    for (s, sz) in groups:
        h = sz // 2
        out[:, s:s + h] = idx[:, s + h:s + sz]
        out[:, s + h:s + sz] = idx[:, s:s + h]
    return out.reshape(-1)


class Packer:
    def __init__(self):
        self.parts = []
        self.off = 0
        self.groups = {}

    def add(self, name, arr):
        arr = np.ascontiguousarray(arr, dtype=np.float32)
        assert arr.shape[0] == 128
        n = arr.shape[1]
        assert n <= 4096, (name, n)
        self.groups[name] = (self.off, n)
        self.parts.append(arr)
        self.off += n


def group_layout():
    off = 0
    g = {}

    def add(name, n):
        nonlocal off
        g[name] = (off, n)
        off += n
    for l in range(L_):
        for i in range(12):
            add(("ada", l, i), 4096)
        add(("naqk", l), 4096)
        add(("nav", l), 2048)
        add(("mlaA", l), 3072)
        add(("mlaB", l), 1536)
        add(("mlaup", l), 2048)
        add(("gqa", l), 4096)
        add(("gqb", l), 4096)
        add(("gk", l), 4096)
        add(("gv", l), 1024)
        for fo in range(8):
            add(("mrg", l, fo), 4096)
        for fo in range(8):
            add(("wout", l, fo), 1024)
        for jj in range(11):
            add(("gu", l, jj), 4096)
        for fo in range(8):
            add(("down", l, fo), 2816)
    return g, off


def pack_weights(inp):
    P = Packer()
    sw_mla = _swap_idx(1, 32, [(0, 32)])
    sw_g = _swap_idx(1, 64, [(0, 32), (32, 32)])
    for l in range(L_):
        ada = inp["ada_w"][l]
        for i in range(12):
            P.add(("ada", l, i), _kc_major(ada[:, i * 512:(i + 1) * 512]))
        w_in = inp["w_in"][l]
        o = 0
        qa = w_in[:, 0:256]; ka = w_in[:, 256:512]; va = w_in[:, 512:768]
        cq = w_in[:, 768:1024]; ckv = w_in[:, 1024:1152]; kr = w_in[:, 1152:1184]
        qc = w_in[:, 1184:1696]; kc_ = w_in[:, 1696:1824]; vc = w_in[:, 1824:1952]
        ga = w_in[:, 1952:2976]; gb = w_in[:, 2976:4000]; gc = w_in[:, 4000:5024]
        P.add(("naqk", l), _kc_major(np.concatenate([qa, ka], axis=1)))
        P.add(("nav", l), _kc_major(va))
        P.add(("mlaA", l), _kc_major(np.concatenate([cq, ckv], axis=1)))
        krA = np.concatenate([kr, kr, kr], axis=1)
        krB = np.concatenate([kr, kr, kr[:, sw_mla]], axis=1)
        P.add(("mlaB", l), _kc_major(np.concatenate([krA, krB], axis=1)))
        uq = inp["mla_w_uq"][l]
        ukv = inp["mla_w_ukv"][l]
        uq_cols = []
        for h in range(4):
            qh = uq[:, h * 96:(h + 1) * 96]
            qh_sw = np.concatenate([qh[:, 0:64], qh[:, 64:96][:, sw_mla]], axis=1)
            uq_cols += [qh, qh_sw]
        uqp = _kc_major(np.concatenate(uq_cols, axis=1))
        kn = np.concatenate([ukv[:, h * 128:h * 128 + 64] for h in range(4)], axis=1)
        vv = np.concatenate([ukv[:, h * 128 + 64:h * 128 + 128] for h in range(4)], axis=1)
        P.add(("mlaup", l), np.concatenate([uqp, kn, vv], axis=1))
        qc_sw = np.concatenate([qc[:, h * 64:(h + 1) * 64][:, sw_g] for h in range(8)], axis=1)
        P.add(("gqa", l), _kc_major(np.concatenate([qc[:, 0:256], qc_sw[:, 0:256]], axis=1)))
        P.add(("gqb", l), _kc_major(np.concatenate([qc[:, 256:512], qc_sw[:, 256:512]], axis=1)))
        k0 = kc_[:, 0:64]; k1 = kc_[:, 64:128]
        P.add(("gk", l), _kc_major(np.concatenate([k0, k0, k1, k1, k0[:, sw_g], k0[:, sw_g], k1[:, sw_g], k1[:, sw_g]], axis=1)))
        P.add(("gv", l), _kc_major(vc))
        bna = inp["w_br_na"][l]; bml = inp["w_br_mla"][l]; bgq = inp["w_br_gqa"][l]
        for fo in range(8):
            cs = slice(fo * 128, (fo + 1) * 128)
            P.add(("mrg", l, fo), np.concatenate([
                _kc_major(ga[:, cs]), _kc_major(gb[:, cs]), _kc_major(gc[:, cs]),
                _kc_major(bna[:, cs]), _kc_major(bml[:, cs]), _kc_major(bgq[:, cs])], axis=1))
        wo = inp["w_out"][l]
        for fo in range(8):
            P.add(("wout", l, fo), _kc_major(wo[:, fo * 128:(fo + 1) * 128]))
        gu = inp["ffn_w_gate_up"][l]
        for jj in range(11):
            parts = []
            for j in (2 * jj, 2 * jj + 1):
                parts.append(_kc_major(gu[:, j * 128:(j + 1) * 128]))
                parts.append(_kc_major(gu[:, HID + j * 128:HID + (j + 1) * 128]))
            P.add(("gu", l, jj), np.concatenate(parts, axis=1))
        wd = inp["ffn_w_down"][l]
        for fo in range(8):
            P.add(("down", l, fo), _kc_major(wd[:, fo * 128:(fo + 1) * 128]))
    g2, tot = group_layout()
    assert tot == P.off and all(g2[k] == P.groups[k] for k in g2)
    return np.concatenate(P.parts, axis=1)


def vec_layout():
    off = 0
    v = {}

    def add(name, n):
        nonlocal off
        v[name] = (off, n)
        off += n
    add("c", 8)
    add("gfin", 8)
    for l in range(L_):
        add(("adab", l), 48)
        add(("gmix", l), 8)
        add(("gffn", l), 8)
        add(("mq", l), 2)
        add(("mkv", l), 1)
        add(("gq", l), 1)
        add(("gqs", l), 1)
        add(("gk", l), 1)
        add(("gks", l), 1)
    return v, off


def pack_vec(inp, b):
    v, n = vec_layout()
    out = np.zeros((128, n), np.float32)

    def put(name, arr):
        o, m = v[name]
        out[:, o:o + m] = arr
    put("c", inp["c"][b].reshape(8, 128).T)
    put("gfin", inp["final_norm_g"].reshape(8, 128).T)
    sw_g = _swap_idx(1, 64, [(0, 32), (32, 32)])
    for l in range(L_):
        put(("adab", l), inp["ada_b"][l].reshape(48, 128).T)
        put(("gmix", l), inp["norm_mix_g"][l].reshape(8, 128).T)
        put(("gffn", l), inp["norm_ffn_g"][l].reshape(8, 128).T)
        put(("mq", l), inp["mla_q_norm_g"][l].reshape(2, 128).T)
        put(("mkv", l), inp["mla_kv_norm_g"][l].reshape(1, 128).T)
        gq = inp["gqa_q_norm_g"][l]; gk = inp["gqa_k_norm_g"][l]
        put(("gq", l), np.concatenate([gq, gq]).reshape(128, 1))
        put(("gqs", l), np.concatenate([gq[sw_g], gq[sw_g]]).reshape(128, 1))
        put(("gk", l), np.concatenate([gk, gk]).reshape(128, 1))
        put(("gks", l), np.concatenate([gk[sw_g], gk[sw_g]]).reshape(128, 1))
    return out


def rope_tables():
    t = np.arange(S_, dtype=np.int32)
    pos_t = t.astype(np.float32)
    pos_r = (t // 64).astype(np.float32)
    pos_c = (t % 64).astype(np.float32)
    inv = (1.0 / (np.float32(10000.0) ** (np.arange(16, dtype=np.float32) / np.float32(16)))).astype(np.float32)

    def cs(pos):
        ang = (pos[None, :] * inv[:, None]).astype(np.float32)
        c = np.cos(ang).astype(np.float32); s = np.sin(ang).astype(np.float32)
        return np.concatenate([c, c], 0), np.concatenate([-s, s], 0)
    tab = np.zeros((4, 128, S_), np.float32)
    cM, sM = cs(pos_t)
    tab[0, 64:96] = cM; tab[1, 64:96] = sM
    cR, sR = cs(pos_r); cC, sC = cs(pos_c)
    cG = np.concatenate([cR, cC], 0); sG = np.concatenate([sR, sC], 0)
    tab[2, 0:64] = cG; tab[2, 64:128] = cG
    tab[3, 0:64] = sG; tab[3, 64:128] = sG
    return tab


NA_PAT = [0, 1] + [2] * 12 + [3, 4]
NA_PAT_M = [0, 1, 2, 14, 15]


def na_kb(m):
    return int(np.clip(2 * m - 4, 0, 22))


def na_bias_tables(rpb):
    out = np.full((L_, 5, 128, 4, 5, 128), NEG, np.float32)
    p = np.arange(128)
    q = np.arange(128)
    for pi, m in enumerate(NA_PAT_M):
        kb = na_kb(m)
        r = 2 * m + q // 64
        qc = q % 64
        rs = np.clip(r - 4, 0, 24)
        ws = np.clip(qc - 8, 0, 48)
        for j in range(5):
            kr = kb + 2 * j + p // 64
            kcol = p % 64
            vr = (kr[:, None] >= rs[None, :]) & (kr[:, None] < rs[None, :] + 8)
            vc = (kcol[:, None] >= ws[None, :]) & (kcol[:, None] < ws[None, :] + 16)
            valid = vr & vc
            ro = np.clip(kr[:, None] - r[None, :] + 7, 0, 14)
            co = np.clip(kcol[:, None] - qc[None, :], -15, 15) + 15
            for l in range(L_):
                for h in range(4):
                    vals = rpb[l, h][ro, co]
                    out[l, pi, :, h, j, :] = np.where(valid, vals, np.float32(NEG))
    return out.reshape(L_, 5, 128, 4 * 640)


def build_program(dbg=None):
    nc = bass.Bass("TRN2", target_bir_lowering=False)
    G, WTOT = group_layout()
    V, NV = vec_layout()
    xin = nc.dram_tensor("xT", [D_, S_], F32, kind="ExternalInput").ap()
    wpk = nc.dram_tensor("wpk", [128, WTOT], F32, kind="ExternalInput").ap()
    vecd = nc.dram_tensor("vec", [128, NV], F32, kind="ExternalInput").ap()
    tabd = nc.dram_tensor("tab", [4, 128, S_], F32, kind="ExternalInput").ap()
    nabd = nc.dram_tensor("nab", [L_ * 5, 128, 2560], F32, kind="ExternalInput").ap()
    identd = nc.dram_tensor("ident", [128, 128], F32, kind="ExternalInput").ap()
    yout = nc.dram_tensor("yT", [D_, S_], F32, kind="ExternalOutput").ap()
    dbgout = None
    if dbg is not None:
        dbgout = nc.dram_tensor("dbg", [128, 16384], F32, kind="ExternalOutput").ap()

    S = Sched(nc)
    R = S.R
    with contextlib.ExitStack() as es:
        def sbt(name, shape, dt):
            return es.enter_context(nc.sbuf_tensor(name, shape, dt))
        xT = sbt("xT_sb", [128, KC * S_], F32)
        hT = sbt("hT_sb", [128, KC * S_], BF16)
        UO = sbt("UO_sb", [128, 34816], BF16)
        Wst = [sbt("W0_sb", [128, 4096], BF16), sbt("W1_sb", [128, 4096], BF16)]
        PT = [sbt("PT%d_sb" % i, [128, 1024], BF16) for i in range(3)]
        TF = [sbt("TF%d_sb" % i, [128, 512], F32) for i in range(4)]
        TB = [sbt("TB%d_sb" % i, [128, 512], BF16) for i in range(2)]
        TABc = [sbt("TABc%d_sb" % i, [128, 512], F32) for i in range(2)]
        TABs = [sbt("TABs%d_sb" % i, [128, 512], F32) for i in range(2)]
        IDENT = sbt("IDENT_sb", [128, 128], BF16)
        VEC = sbt("VEC_sb", [128, NV], F32)
        MODS = [sbt("MOD%d_sb" % i, [128, 64], F32) for i in range(L_)]
        cur = {"l": 0}
        SCB = sbt("SCB_sb", [128, 8], BF16)
        SMALL = sbt("SMALL_sb", [128, 16], F32)
        ONES = sbt("ONES_sb", [128, 128], BF16)
        BONES = sbt("BONES_sb", [128, 128], BF16)
        EPSC = sbt("EPS_sb", [128, 1], F32)
        PS = es.enter_context(nc.psum_tensor("PS", [128, 4096], F32))

        OT0 = 0
        U0 = 16384

        def oT(c, t0, t1, rows=slice(0, 128)):
            return UO[rows, OT0 + c * S_ + t0: OT0 + c * S_ + t1]

        def U(off, n, rows=slice(0, 128)):
            return UO[rows, U0 + off: U0 + off + n]

        def bank(i):
            return PS[:, i * 512:(i + 1) * 512]

        state = {"bank": 0, "pair": 0, "pt": 0, "tf": 0, "tb": 0, "w": 0, "tab": 0, "evac": 0}

        def nbank():
            b = state["bank"]; state["bank"] = (b + 1) % 8
            return b

        def ntf():
            i = state["tf"]; state["tf"] = (i + 1) % 4
            return i

        def ntb():
            i = state["tb"]; state["tb"] = (i + 1) % 2
            return i

        dsem_w = [S.new_dma_sem(), S.new_dma_sem()]
        dsem_xs = [S.new_dma_sem() for _ in range(NTC)]
        dsem_vec = S.new_dma_sem()
        dsem_tab = [S.new_dma_sem(), S.new_dma_sem()]
        dsem_bias = S.new_dma_sem()
        dsem_out = [S.new_dma_sem() for _ in range(4)]
        dsem_id = S.new_dma_sem()

        def loadw(gname):
            off, n = G[gname]
            i = state["w"]; state["w"] = 1 - i
            S.dma("pool", lambda e, i=i, off=off, n=n: e.dma_start(out=Wst[i][:, 0:n], in_=wpk[:, off:off + n]),
                  writes=[R("W", i)], sem=dsem_w[i])
            return i

        steps = []

        extra_q = []

        def step(g, fn, popx=False):
            steps.append((g, fn))
            if (g is not None or popx) and extra_q and not state.get("in_extra"):
                state["in_extra"] = True
                extra_q.pop(0)()
                state["in_extra"] = False

        def flush_extras():
            while extra_q:
                state["in_extra"] = True
                extra_q.pop(0)()
                state["in_extra"] = False

        def run_steps():
            nxt = {}
            widx_for = {}
            order = [i for i, (g, _) in enumerate(steps) if g is not None]
            pos = 0
            if order:
                widx_for[order[0]] = loadw(steps[order[0]][0])
            for i, (g, fn) in enumerate(steps):
                if g is not None:
                    cur = widx_for[i]
                    pos += 1
                    if pos < len(order):
                        widx_for[order[pos]] = loadw(steps[order[pos]][0])
                    fn(cur)
                else:
                    fn(None)
            steps.clear()

        def Wap(i, off, n, rows=slice(0, 128)):
            return Wst[i][rows, off:off + n]

        def mm(out, lhsT, rhs, start, stop, reads, writes):
            S.op("pe", lambda e: e.matmul(out, lhsT, rhs, start=start, stop=stop), reads=reads, writes=writes)

        def evac_copy(out, in_, reads, writes):
            k = state["evac"]; state["evac"] = 1 - k
            if k == 0:
                S.op("dve", lambda e: e.tensor_copy(out=out, in_=in_), reads=reads, writes=writes)
            else:
                S.op("act", lambda e: e.activation(out=out, in_=in_, func=AF.Copy), reads=reads, writes=writes)

        def vcol(name, j=0, n=1):
            o, m = V[name]
            return VEC[:, o + j:o + j + n]

        S.dma("sp", lambda e: e.dma_start(out=VEC[:], in_=vecd), writes=[R("VEC")], sem=dsem_vec)
        for tc in range(NTC):
            for kc in range(KC):
                S.dma("sp", lambda e: e.dma_start(out=xT[:, kc * S_ + tc * 512: kc * S_ + tc * 512 + 512],
                                                  in_=xin[kc * 128:(kc + 1) * 128, tc * 512:(tc + 1) * 512]),
                      writes=[R("x", kc, tc)], sem=dsem_xs[tc])
        S.dma("pool", lambda e: e.dma_start(out=IDENT[:], in_=identd), writes=[R("IDENT")], sem=dsem_id)
        S.op("dve", lambda e: e.memset(ONES[:], 1.0), writes=[R("ONES")])
        S.op("dve", lambda e: e.memset(BONES[:], 0.0), writes=[R("BONES")])
        S.op("dve", lambda e: e.memset(BONES[0:64, 0:64], 1.0), writes=[R("BONES")])
        S.op("dve", lambda e: e.memset(BONES[64:128, 64:128], 1.0), writes=[R("BONES")])
        S.op("dve", lambda e: e.memset(EPSC[:], EPS), writes=[R("EPS")])
        S.op("act", lambda e: e.activation(out=SMALL[:, 0:8], in_=vcol("c", 0, 8), func=AF.Tanh, scale=0.5),
             reads=[R("VEC")], writes=[R("SMALL")])
        S.op("dve", lambda e: e.scalar_tensor_tensor(out=SMALL[:, 8:16], in0=SMALL[:, 0:8], scalar=1.0, in1=vcol("c", 0, 8),
                                                     op0=ALU.add, op1=ALU.mult), reads=[R("SMALL"), R("VEC")], writes=[R("SMALL2")])
        S.op("dve", lambda e: e.tensor_scalar(out=SCB[:], in0=SMALL[:, 8:16], scalar1=0.5, scalar2=None, op0=ALU.mult),
             reads=[R("SMALL2")], writes=[R("SCB")])

        def ada_step(l, i):
            def f(w):
                b = state["ada_bank"]() if state.get("ada_bank") else nbank()
                for fo4 in range(4):
                    for kc in range(KC):
                        mm(PS[:, b * 512 + fo4: b * 512 + fo4 + 1], Wap(w, kc * 512 + fo4 * 128, 128), SCB[:, kc:kc + 1],
                           kc == 0, kc == KC - 1, [R("W", w), R("SCB")], [R("ps", b)])
                col = (i // 2) * 8 + (i % 2) * 4
                if state.get("ada_bank"):
                    for c4 in range(4):
                        S.op("act", lambda e: e.activation(out=MODS[l][:, col + c4:col + c4 + 1], in_=PS[:, b * 512 + c4:b * 512 + c4 + 1],
                                                           func=AF.Identity, bias=vcol(("adab", l), col + c4), scale=1.0),
                             reads=[R("ps", b), R("VEC")], writes=[R("MOD", l)])
                else:
                    S.op("dve", lambda e: e.tensor_tensor(
                        out=MODS[l][:, col:col + 4], in0=PS[:, b * 512:b * 512 + 4], in1=vcol(("adab", l), col, 4), op=ALU.add),
                        reads=[R("ps", b), R("VEC")], writes=[R("MOD", l)])
            step(("ada", l, i), f)

        def ada_fin_m(l):
            def g(_):
                S.op("dve", lambda e: e.scalar_tensor_tensor(out=MODS[l][:, 48:56], in0=MODS[l][:, 8:16], scalar=1.0, in1=vcol(("gmix", l), 0, 8),
                                                             op0=ALU.add, op1=ALU.mult), reads=[R("MOD", l), R("VEC")], writes=[R("MOD", l)])
            step(None, g)

        def ada_fin_rest(l):
            def g(_):
                S.op("dve", lambda e: e.scalar_tensor_tensor(out=MODS[l][:, 56:64], in0=MODS[l][:, 32:40], scalar=1.0, in1=vcol(("gffn", l), 0, 8),
                                                             op0=ALU.add, op1=ALU.mult), reads=[R("MOD", l), R("VEC")], writes=[R("MOD", l)])
                S.op("dve", lambda e: e.tensor_scalar(out=MODS[l][:, 16:24], in0=MODS[l][:, 16:24], scalar1=0.5, scalar2=None, op0=ALU.mult),
                     reads=[R("MOD", l)], writes=[R("MOD", l)])
                S.op("dve", lambda e: e.tensor_scalar(out=MODS[l][:, 40:48], in0=MODS[l][:, 40:48], scalar1=0.5, scalar2=None, op0=ALU.mult),
                     reads=[R("MOD", l)], writes=[R("MOD", l)])
            step(None, g)

        def rstd_from_bank(b, nfeat, tfi, rows=slice(0, 128), n=512):
            S.op("act", lambda e: e.activation(out=TF[tfi][rows, 0:n], in_=PS[rows, b * 512:b * 512 + n], func=AF.Sqrt,
                                               bias=EPSC[rows, 0:1], scale=1.0 / nfeat),
                 reads=[R("ps", b), R("EPS")], writes=[R("TF", tfi)])
            S.op("dve", lambda e: e.reciprocal(out=TF[tfi][rows, 0:n], in_=TF[tfi][rows, 0:n]),
                 reads=[R("TF", tfi)], writes=[R("TF", tfi)])

        def norm_tc(l, gcol, shcol, tc):
            def f(_):
                M_ = MODS[l]
                t0 = tc * 512
                b = nbank()
                for kc in range(KC):
                    ti = ntb()
                    S.op("act", lambda e: e.activation(out=TB[ti][:, 0:512], in_=xT[:, kc * S_ + t0: kc * S_ + t0 + 512], func=AF.Square),
                         reads=[R("x", kc, tc)], writes=[R("TB", ti)])
                    mm(bank(b), ONES[:, :], TB[ti][:, 0:512], kc == 0, kc == KC - 1, [R("ONES"), R("TB", ti)], [R("ps", b)])
                rt = ntf()
                rstd_from_bank(b, float(D_), rt)
                for kc in range(KC):
                    t2 = ntf()
                    if t2 == rt:
                        t2 = ntf()
                    S.op("dve", lambda e: e.scalar_tensor_tensor(
                        out=TF[t2][:, 0:512], in0=xT[:, kc * S_ + t0: kc * S_ + t0 + 512], scalar=M_[:, gcol + kc: gcol + kc + 1],
                        in1=TF[rt][:, 0:512], op0=ALU.mult, op1=ALU.mult),
                        reads=[R("x", kc, tc), R("MOD", l), R("TF", rt)], writes=[R("TF", t2)])
                    S.op("act", lambda e: e.activation(
                        out=hT[:, kc * S_ + t0: kc * S_ + t0 + 512], in_=TF[t2][:, 0:512], func=AF.Identity,
                        bias=M_[:, shcol + kc: shcol + kc + 1], scale=1.0),
                        reads=[R("TF", t2), R("MOD", l)], writes=[R("h", kc, tc)])
            step(None, f)

        def hsl(kc, t0, n):
            return hT[:, kc * S_ + t0: kc * S_ + t0 + n]

        def hres(tc):
            return [R("h", kc, tc) for kc in range(KC)]

        VOFF = 8192 + 4096

        def vaug_ones(voff, layoutB):
            cols = [(0, 64), (128, 64), (192, 64), (320, 64)] if layoutB else [(64, 64), (256, 64)]
            def f(_):
                for tt in range(16):
                    for (c0, n) in cols:
                        S.op("dve", lambda e, tt=tt, c0=c0, n=n: e.memset(U(voff + tt * 384 + c0, n), 1.0), writes=[R("V", tt)])
            step(None, f)

        def vaug(voff, tt, c0):
            return U(voff + tt * 384 + c0, 128)

        def attention(qT, kT, qres, kres, vfn, scale, odd, outfn, outres, before_tc=None, after_tc=None, npairs=3, depth=2):
            orow = slice(64, 128) if odd else slice(0, 64)
            srow = slice(0, 64) if odd else slice(64, 128)
            for tc in range(NTC):
                if before_tc is not None:
                    before_tc(tc)
                acc = 6 + (state["pair"] % 2)
                state["pair"] += 1
                pairs = list(range(npairs))

                def qk(jp):
                    pb = pairs[jp % npairs]
                    for jj in range(2):
                        j = 2 * jp + jj
                        b = 2 * pb + jj
                        mm(bank(b), kT(j), qT(tc), True, True, [kres(j), qres(tc)], [R("ps", b)])
                for d_ in range(depth):
                    qk(d_)
                for jp in range(8):
                    if jp + depth < 8:
                        qk(jp + depth)
                    pb = pairs[jp % npairs]
                    pi = state["pt"]; state["pt"] = (pi + 1) % 3
                    S.op("act", lambda e, pb=pb, pi=pi: e.activation(out=PT[pi][:, 0:1024], in_=PS[:, pb * 1024:(pb + 1) * 1024], func=AF.Exp, scale=scale),
                         reads=[R("ps", 2 * pb), R("ps", 2 * pb + 1)], writes=[R("PT", pi)])
                    for jj in range(2):
                        j = 2 * jp + jj
                        mm(bank(acc), vfn(j), PT[pi][:, jj * 512:(jj + 1) * 512], j == 0, j == 15, [R("V", j), R("PT", pi)], [R("ps", acc)])
                ti = ntf()
                S.op("act", lambda e, ti=ti, acc=acc: e.activation(out=TF[ti][orow, 0:512], in_=PS[srow, acc * 512:(acc + 1) * 512], func=AF.Copy),
                     reads=[R("ps", acc)], writes=[R("TF", ti)])
                S.op("dve", lambda e, ti=ti: e.reciprocal(out=TF[ti][orow, 0:512], in_=TF[ti][orow, 0:512]), reads=[R("TF", ti)], writes=[R("TF", ti)])
                S.op("dve", lambda e, ti=ti, acc=acc, tc=tc: e.tensor_tensor(out=outfn(tc, orow), in0=PS[orow, acc * 512:(acc + 1) * 512],
                                                                          in1=TF[ti][orow, 0:512], op=ALU.mult),
                     reads=[R("ps", acc), R("TF", ti)], writes=[outres(tc)])
                if after_tc is not None:
                    after_tc(tc)

        def na_branch(l):
            QO, KO, VO = 0, 4096, 8192

            def proj(w):
                for ti in range(4):
                    dst = QO if ti < 2 else KO
                    p = ti % 2
                    for tc in range(NTC):
                        b = nbank()
                        for kc in range(KC):
                            mm(bank(b), Wap(w, kc * 512 + ti * 128, 128), hsl(kc, tc * 512, 512), kc == 0, kc == KC - 1,
                               [R("W", w), R("h", kc, tc)], [R("ps", b)])
                        evac_copy(U(dst + p * S_ + tc * 512, 512), bank(b), [R("ps", b)], [R("QK", ti, tc)])
            step(("naqk", l), proj)
            vaug_ones(VO, False)

            def projv(w):
                for tt in range(16):
                    b = nbank()
                    for kc in range(KC):
                        mm(PS[:, b * 512: b * 512 + 256], hsl(kc, tt * 128, 128), Wap(w, kc * 256, 256), kc == 0, kc == KC - 1,
                           [R("W", w), R("h", kc, tt // 4)], [R("ps", b)])
                    for (s0, n, d0) in [(0, 64, 0), (64, 128, 128), (192, 64, 320)]:
                        evac_copy(U(VO + tt * 384 + d0, n), PS[:, b * 512 + s0: b * 512 + s0 + n], [R("ps", b)], [R("V", tt)])
            step(("nav", l), projv)

            def attn(_):
                items = [(m, h) for m in range(16) for h in range(4)]
                n = len(items)
                st = {}

                BOFF = 14336

                def stA(i):
                    m, h = items[i]
                    if h == 0 and (m == 0 or NA_PAT[m] != NA_PAT[m - 1]):
                        pat = NA_PAT[m]
                        S.dma("pool", lambda e: e.dma_start(out=U(BOFF, 2560), in_=nabd[l * 5 + pat]), writes=[R("BIAS")], sem=dsem_bias)
                        S.op("dve", lambda e: e.tensor_scalar(out=U(BOFF, 2560), in0=U(BOFF, 2560), scalar1=8.0, scalar2=None, op0=ALU.mult),
                             reads=[R("BIAS")], writes=[R("BIAS")])
                    kt0 = na_kb(m) // 2
                    p = h // 2
                    rows = slice((h % 2) * 64, (h % 2) * 64 + 64)
                    pb = i % 3
                    for j in range(5):
                        col = pb * 1024 + j * 128
                        mm(PS[:, col:col + 128], U(KO + p * S_ + (kt0 + j) * 128, 128, rows), U(QO + p * S_ + m * 128, 128, rows), True, False,
                           [R("QK", 2 + p, (kt0 + j) // 4), R("QK", p, m // 4)], [R("ps", 2 * pb + (j // 4))])
                        mm(PS[:, col:col + 128], IDENT[:, :], U(BOFF + h * 640 + j * 128, 128), False, True,
                           [R("IDENT"), R("BIAS")], [R("ps", 2 * pb + (j // 4))])

                def stB(i):
                    m, h = items[i]
                    kt0 = na_kb(m) // 2
                    pb = i % 3
                    pi = i % 3
                    S.op("act", lambda e: e.activation(out=PT[pi][:, 0:640], in_=PS[:, pb * 1024: pb * 1024 + 640], func=AF.Exp, scale=0.125),
                         reads=[R("ps", 2 * pb), R("ps", 2 * pb + 1)], writes=[R("PT", pi)])
                    acc = 6 + (i % 2)
                    c0 = [0, 64, 192, 256][h]
                    for j in range(5):
                        mm(PS[:, acc * 512: acc * 512 + 128], vaug(VO, kt0 + j, c0), PT[pi][:, j * 128:(j + 1) * 128], j == 0, j == 4,
                           [R("V", kt0 + j), R("PT", pi)], [R("ps", acc)])

                def stC(i):
                    m, h = items[i]
                    p = h // 2
                    odd = (h % 2) == 1
                    acc = 6 + (i % 2)
                    orow = slice(64, 128) if odd else slice(0, 64)
                    srow = slice(0, 64) if odd else slice(64, 128)
                    t2 = ntf()
                    S.op("act", lambda e: e.activation(
                        out=TF[t2][orow, 0:128], in_=PS[srow, acc * 512: acc * 512 + 128], func=AF.Copy),
                        reads=[R("ps", acc)], writes=[R("TF", t2)])
                    S.op("dve", lambda e: e.reciprocal(out=TF[t2][orow, 0:128], in_=TF[t2][orow, 0:128]),
                         reads=[R("TF", t2)], writes=[R("TF", t2)])
                    S.op("dve", lambda e: e.tensor_tensor(
                        out=oT(p, m * 128, m * 128 + 128, orow), in0=PS[orow, acc * 512: acc * 512 + 128], in1=TF[t2][orow, 0:128], op=ALU.mult),
                        reads=[R("ps", acc), R("TF", t2)], writes=[R("o", p, m // 4)])

                for i in range(n + 2):
                    if i < n:
                        stA(i)
                    if 0 <= i - 1 < n:
                        stB(i - 1)
                    if 0 <= i - 2 < n:
                        stC(i - 2)
            step(None, attn)

        def load_tab(ci, si, tc, rows):
            k = state["tab"]; state["tab"] = 1 - k
            S.dma("sp", lambda e, k=k: e.dma_start(out=TABc[k][rows, :], in_=tabd[ci, rows, tc * 512:(tc + 1) * 512]),
                  writes=[R("TABc", k)], sem=dsem_tab[k])
            S.dma("sp", lambda e, k=k: e.dma_start(out=TABs[k][rows, :], in_=tabd[si, rows, tc * 512:(tc + 1) * 512]),
                  writes=[R("TABs", k)], sem=dsem_tab[k])
            return k

        def rope_combine(bA, bB, rows, k, gA, gB, out_ap, out_res, rstd_tf=None):
            t1 = ntf()
            while rstd_tf is not None and t1 == rstd_tf:
                t1 = ntf()
            t2 = ntf()
            while (rstd_tf is not None and t2 == rstd_tf) or t2 == t1:
                t2 = ntf()
            if gA is None:
                S.op("dve", lambda e: e.tensor_tensor(out=TF[t1][rows, 0:512], in0=PS[rows, bA * 512:(bA + 1) * 512], in1=TABc[k][rows, :], op=ALU.mult),
                     reads=[R("ps", bA), R("TABc", k)], writes=[R("TF", t1)])
                S.op("dve", lambda e: e.tensor_tensor(out=TF[t2][rows, 0:512], in0=PS[rows, bB * 512:(bB + 1) * 512], in1=TABs[k][rows, :], op=ALU.mult),
                     reads=[R("ps", bB), R("TABs", k)], writes=[R("TF", t2)])
            else:
                S.op("dve", lambda e: e.scalar_tensor_tensor(out=TF[t1][rows, 0:512], in0=PS[rows, bA * 512:(bA + 1) * 512], scalar=gA,
                                                             in1=TABc[k][rows, :], op0=ALU.mult, op1=ALU.mult),
                     reads=[R("ps", bA), R("TABc", k), R("VEC")], writes=[R("TF", t1)])
                S.op("dve", lambda e: e.scalar_tensor_tensor(out=TF[t2][rows, 0:512], in0=PS[rows, bB * 512:(bB + 1) * 512], scalar=gB,
                                                             in1=TABs[k][rows, :], op0=ALU.mult, op1=ALU.mult),
                     reads=[R("ps", bB), R("TABs", k), R("VEC")], writes=[R("TF", t2)])
            if rstd_tf is None:
                S.op("dve", lambda e: e.tensor_tensor(out=out_ap, in0=TF[t1][rows, 0:512], in1=TF[t2][rows, 0:512], op=ALU.add),
                     reads=[R("TF", t1), R("TF", t2)], writes=[out_res])
            else:
                S.op("dve", lambda e: e.tensor_tensor(out=TF[t1][rows, 0:512], in0=TF[t1][rows, 0:512], in1=TF[t2][rows, 0:512], op=ALU.add),
                     reads=[R("TF", t1), R("TF", t2)], writes=[R("TF", t1)])
                S.op("dve", lambda e: e.tensor_tensor(out=out_ap, in0=TF[t1][rows, 0:512], in1=TF[rstd_tf][rows, 0:512], op=ALU.mult),
                     reads=[R("TF", t1), R("TF", rstd_tf)], writes=[out_res])

        def mla_branch(l):
            CQ, CKV, KPE, VO, QH, KH = 0, 4096, 6144, 8192, 14336, 16384

            def projA(w):
                for tc in range(NTC):
                    bs = []
                    for ti in range(3):
                        b = nbank(); bs.append(b)
                        for kc in range(KC):
                            mm(bank(b), Wap(w, kc * 384 + ti * 128, 128), hsl(kc, tc * 512, 512), kc == 0, kc == KC - 1,
                               [R("W", w), R("h", kc, tc)], [R("ps", b)])
                    for (tis, nfeat, vname, dst) in [((0, 1), 256.0, ("mq", l), CQ), ((2,), 128.0, ("mkv", l), CKV)]:
                        bS = nbank()
                        for n_, ti in enumerate(tis):
                            tb = ntb()
                            S.op("act", lambda e, tb=tb, b=bs[ti]: e.activation(out=TB[tb][:, 0:512], in_=bank(b), func=AF.Square),
                                 reads=[R("ps", bs[ti])], writes=[R("TB", tb)])
                            mm(bank(bS), ONES[:, :], TB[tb][:, 0:512], n_ == 0, n_ == len(tis) - 1, [R("ONES"), R("TB", tb)], [R("ps", bS)])
                        rt = ntf()
                        rstd_from_bank(bS, nfeat, rt)
                        for n_, ti in enumerate(tis):
                            S.op("dve", lambda e, n_=n_, b=bs[ti], rt=rt, dst=dst, vname=vname: e.scalar_tensor_tensor(
                                out=U(dst + n_ * S_ + tc * 512, 512), in0=bank(b), scalar=vcol(vname, n_), in1=TF[rt][:, 0:512],
                                op0=ALU.mult, op1=ALU.mult), reads=[R("ps", bs[ti]), R("TF", rt), R("VEC")], writes=[R("C", dst, n_, tc)])
            step(("mlaA", l), projA)

            def projB(w):
                rows = slice(64, 96)
                knext = load_tab(0, 1, 0, rows)
                for tc in range(NTC):
                    k = knext
                    if tc + 1 < NTC:
                        knext = load_tab(0, 1, tc + 1, rows)
                    bA = nbank(); bB = nbank()
                    for (b, c0) in [(bA, 0), (bB, 96)]:
                        for kc in range(KC):
                            mm(PS[0:96, b * 512:(b + 1) * 512], Wap(w, kc * 192 + c0, 96), hsl(kc, tc * 512, 512), kc == 0, kc == KC - 1,
                               [R("W", w), R("h", kc, tc)], [R("ps", b)])
                    rope_combine(bA, bB, rows, k, None, None, U(KPE + tc * 512, 512, rows), R("KPE", tc))
            step(("mlaB", l), projB)
            vaug_ones(VO, False)

            def up(w):
                UQ, KN, VV = 0, 1536, 1792
                for tt in range(16):
                    b = nbank()
                    mm(PS[:, b * 512: b * 512 + 256], U(CKV + tt * 128, 128), Wap(w, VV, 256), True, True,
                       [R("W", w), R("C", CKV, 0, tt // 4)], [R("ps", b)])
                    for (s0, n, d0) in [(0, 64, 0), (64, 128, 128), (192, 64, 320)]:
                        evac_copy(U(VO + tt * 384 + d0, n), PS[:, b * 512 + s0: b * 512 + s0 + n], [R("ps", b)], [R("V", tt)])
                def emit_K(h):
                    for tc in range(NTC):
                        b = nbank()
                        mm(PS[0:64, b * 512:(b + 1) * 512], Wap(w, KN + h * 64, 64), U(CKV + tc * 512, 512), True, True,
                           [R("W", w), R("C", CKV, 0, tc)], [R("ps", b)])
                        evac_copy(U(KH + tc * 512, 512, slice(0, 64)), PS[0:64, b * 512:(b + 1) * 512], [R("ps", b)], [R("KH", tc)])
                        S.op("dve", lambda e: e.tensor_copy(out=U(KH + tc * 512, 512, slice(64, 96)), in_=U(KPE + tc * 512, 512, slice(64, 96))),
                             reads=[R("KPE", tc)], writes=[R("KH", tc)])

                def emit_Q(h, tc, k, banks=None):
                    if banks is None:
                        bA = nbank(); bB = nbank()
                    else:
                        bA, bB = banks
                    for (b2, c0) in [(bA, h * 192), (bB, h * 192 + 96)]:
                        for c in range(2):
                            mm(PS[0:96, b2 * 512:(b2 + 1) * 512], Wap(w, UQ + c * 768 + c0, 96), U(CQ + c * S_ + tc * 512, 512), c == 0, c == 1,
                               [R("W", w), R("C", CQ, c, tc)], [R("ps", b2)])
                    evac_copy(U(QH + tc * 512, 512, slice(0, 64)), PS[0:64, bA * 512:(bA + 1) * 512], [R("ps", bA)], [R("QH", tc)])
                    rope_combine(bA, bB, slice(64, 96), k, None, None, U(QH + tc * 512, 512, slice(64, 96)), R("QH", tc))

                for tc in range(NTC):
                    S.op("dve", lambda e: e.memset(U(QH + tc * 512, 512, slice(64, 128)), 0.0), writes=[R("QH", tc)])
                    S.op("dve", lambda e: e.memset(U(KH + tc * 512, 512, slice(64, 128)), 0.0), writes=[R("KH", tc)])
                knext = load_tab(0, 1, 0, slice(64, 96))
                for tc in range(NTC):
                    k = knext
                    if tc + 1 < NTC:
                        knext = load_tab(0, 1, tc + 1, slice(64, 96))
                    emit_Q(0, tc, k)
                for h in range(4):
                    emit_K(h)
                    odd = (h % 2) == 1
                    c0 = [0, 64, 192, 256][h]
                    tabk = {}
                    if h + 1 < 4:
                        def before_tc(tc, tabk=tabk):
                            tabk[tc] = load_tab(0, 1, tc, slice(64, 96))

                        def after_tc(tc, tabk=tabk, h=h):
                            emit_Q(h + 1, tc, tabk[tc], banks=(4, 5))
                    else:
                        before_tc = None
                        after_tc = None
                    attention(lambda tc: U(QH + tc * 512, 512, slice(0, 128)),
                              lambda j: U(KH + j * 128, 128, slice(0, 128)),
                              lambda tc: R("QH", tc), lambda j: R("KH", j // 4),
                              lambda j, c0=c0: vaug(VO, j, c0), float(96 ** -0.5), odd,
                              lambda tc, orow, h=h: oT(2 + h // 2, tc * 512, tc * 512 + 512, orow),
                              lambda tc, h=h: R("o", 2 + h // 2, tc), before_tc=before_tc, after_tc=after_tc, npairs=2, depth=1)
            step(("mlaup", l), up)

        def gqa_branch(l, before_attn=None, after_attn=None):
            QO, KO, VO = 0, 8192, 12288

            def qk_group(w, tiles, gname, gsname):
                knext = load_tab(2, 3, 0, slice(0, 128))
                for tc in range(NTC):
                    k = knext
                    if tc + 1 < NTC:
                        knext = load_tab(2, 3, tc + 1, slice(0, 128))
                    for (wc0, dst_off, resname, ridx) in tiles:
                        bA = nbank(); bB = nbank()
                        for (b, c0) in [(bA, wc0), (bB, wc0 + 256)]:
                            for kc in range(KC):
                                mm(bank(b), Wap(w, kc * 512 + c0, 128), hsl(kc, tc * 512, 512), kc == 0, kc == KC - 1,
                                   [R("W", w), R("h", kc, tc)], [R("ps", b)])
                        tb = ntb()
                        S.op("act", lambda e: e.activation(out=TB[tb][:, 0:512], in_=bank(bA), func=AF.Square),
                             reads=[R("ps", bA)], writes=[R("TB", tb)])
                        bS = nbank()
                        mm(bank(bS), BONES[:, :], TB[tb][:, 0:512], True, True, [R("BONES"), R("TB", tb)], [R("ps", bS)])
                        rt = ntf()
                        rstd_from_bank(bS, 64.0, rt)
                        rope_combine(bA, bB, slice(0, 128), k, vcol((gname, l)), vcol((gsname, l)),
                                     U(dst_off + tc * 512, 512), R(resname, ridx, tc), rstd_tf=rt)

            step(("gqa", l), lambda w: qk_group(w, [(p * 128, QO + p * S_, "GQ", p) for p in range(2)], "gq", "gqs"))
            step(("gqb", l), lambda w: qk_group(w, [(p * 128, QO + (2 + p) * S_, "GQ", 2 + p) for p in range(2)], "gq", "gqs"))
            step(("gk", l), lambda w: qk_group(w, [(g * 128, KO + g * S_, "GK", g) for g in range(2)], "gk", "gks"))
            vaug_ones(VO, True)

            def pv(w):
                for tt in range(16):
                    b = nbank()
                    for kc in range(KC):
                        mm(PS[:, b * 512: b * 512 + 128], hsl(kc, tt * 128, 128), Wap(w, kc * 128, 128), kc == 0, kc == KC - 1,
                           [R("W", w), R("h", kc, tt // 4)], [R("ps", b)])
                    for (s0, d0) in [(0, 64), (64, 256)]:
                        evac_copy(U(VO + tt * 384 + d0, 64), PS[:, b * 512 + s0: b * 512 + s0 + 64], [R("ps", b)], [R("V", tt)])
            step(("gv", l), pv)

            def attn_pair_tc(p, tc):
                def f(_):
                    g = p // 2
                    vA = g * 192 + 64
                    vB = g * 192
                    accA = 4 + 2 * (state["pair"] % 2)
                    accB = accA + 1
                    state["pair"] += 1

                    def qk(j):
                        pb = j % 2
                        for hh in range(2):
                            rows = slice(hh * 64, hh * 64 + 64)
                            mm(bank(2 * pb + hh), U(KO + g * S_ + j * 128, 128, rows), U(QO + p * S_ + tc * 512, 512, rows), True, True,
                               [R("GK", g, j // 4), R("GQ", p, tc)], [R("ps", 2 * pb + hh)])
                    qk(0)
                    for j in range(16):
                        if j + 1 < 16:
                            qk(j + 1)
                        pb = j % 2
                        pi = state["pt"]; state["pt"] = (pi + 1) % 3
                        S.op("act", lambda e: e.activation(out=PT[pi][:, 0:1024], in_=PS[:, pb * 1024:(pb + 1) * 1024], func=AF.Exp, scale=0.125),
                             reads=[R("ps", 2 * pb), R("ps", 2 * pb + 1)], writes=[R("PT", pi)])
                        mm(bank(accA), vaug(VO, j, vA), PT[pi][:, 0:512], j == 0, j == 15, [R("V", j), R("PT", pi)], [R("ps", accA)])
                        mm(bank(accB), vaug(VO, j, vB), PT[pi][:, 512:1024], j == 0, j == 15, [R("V", j), R("PT", pi)], [R("ps", accB)])
                    ti = ntf()
                    for (acc, orow, srow) in [(accA, slice(0, 64), slice(64, 128)), (accB, slice(64, 128), slice(0, 64))]:
                        S.op("act", lambda e: e.activation(out=TF[ti][orow, 0:512], in_=PS[srow, acc * 512:(acc + 1) * 512], func=AF.Copy),
                             reads=[R("ps", acc)], writes=[R("TF", ti)])
                    S.op("dve", lambda e: e.reciprocal(out=TF[ti][:, 0:512], in_=TF[ti][:, 0:512]), reads=[R("TF", ti)], writes=[R("TF", ti)])
                    for (acc, orow, srow) in [(accA, slice(0, 64), slice(64, 128)), (accB, slice(64, 128), slice(0, 64))]:
                        S.op("dve", lambda e: e.tensor_tensor(out=oT(4 + p, tc * 512, tc * 512 + 512, orow), in0=PS[orow, acc * 512:(acc + 1) * 512],
                                                              in1=TF[ti][orow, 0:512], op=ALU.mult),
                             reads=[R("ps", acc), R("TF", ti)], writes=[R("o", 4 + p, tc)])
                step(None, f, popx=True)
            if before_attn is not None:
                before_attn()
            step(None, lambda _: state.__setitem__("ada_bank", lambda: 4 + 2 * (state["pair"] % 2)))
            for p in range(4):
                for tc in range(NTC):
                    attn_pair_tc(p, tc)
            step(None, lambda _: state.__setitem__("ada_bank", None))
            if after_attn is not None:
                after_attn()

        def merge(l):
            MO = 0
            brk = [(0, 2), (2, 2), (4, 4)]
            for fo in range(8):
                def f(w, fo=fo):
                    for tc in range(NTC):
                        acc = ntf()
                        boff = 3072
                        for br in range(3):
                            bG = nbank(); bY = nbank()
                            for kc in range(KC):
                                mm(bank(bG), Wap(w, br * 1024 + kc * 128, 128), hsl(kc, tc * 512, 512), kc == 0, kc == KC - 1,
                                   [R("W", w), R("h", kc, tc)], [R("ps", bG)])
                            c0, ncb = brk[br]
                            for c in range(ncb):
                                mm(bank(bY), Wap(w, boff + c * 128, 128), oT(c0 + c, tc * 512, tc * 512 + 512), c == 0, c == ncb - 1,
                                   [R("W", w), R("o", c0 + c, tc)], [R("ps", bY)])
                            boff += ncb * 128
                            th = ntf()
                            while th == acc:
                                th = ntf()
                            S.op("act", lambda e, th=th, bG=bG: e.activation(out=TF[th][:, 0:512], in_=bank(bG), func=AF.Tanh, scale=0.5),
                                 reads=[R("ps", bG)], writes=[R("TF", th)])
                            if br == 0:
                                S.op("dve", lambda e, th=th, bY=bY, acc=acc: e.scalar_tensor_tensor(
                                    out=TF[acc][:, 0:512], in0=TF[th][:, 0:512], scalar=1.0, in1=bank(bY), op0=ALU.add, op1=ALU.mult),
                                    reads=[R("TF", th), R("ps", bY)], writes=[R("TF", acc)])
                            else:
                                S.op("dve", lambda e, th=th, bY=bY: e.scalar_tensor_tensor(
                                    out=TF[th][:, 0:512], in0=TF[th][:, 0:512], scalar=1.0, in1=bank(bY), op0=ALU.add, op1=ALU.mult),
                                    reads=[R("TF", th), R("ps", bY)], writes=[R("TF", th)])
                                if br == 1:
                                    S.op("dve", lambda e, th=th, acc=acc: e.tensor_tensor(out=TF[acc][:, 0:512], in0=TF[acc][:, 0:512], in1=TF[th][:, 0:512], op=ALU.add),
                                         reads=[R("TF", th), R("TF", acc)], writes=[R("TF", acc)])
                                else:
                                    S.op("dve", lambda e, th=th, acc=acc, tc=tc: e.tensor_tensor(
                                        out=U(MO + fo * S_ + tc * 512, 512), in0=TF[acc][:, 0:512], in1=TF[th][:, 0:512], op=ALU.add),
                                        reads=[R("TF", th), R("TF", acc)], writes=[R("M", fo, tc)])
                step(("mrg", l, fo), f)
            for fo2 in range(8):
                def f2(w, fo2=fo2):
                    for tc in range(NTC):
                        b = nbank()
                        for fo in range(8):
                            mm(bank(b), Wap(w, fo * 128, 128), U(MO + fo * S_ + tc * 512, 512), fo == 0, fo == 7,
                               [R("W", w), R("M", fo, tc)], [R("ps", b)])
                        xs = xT[:, fo2 * S_ + tc * 512: fo2 * S_ + tc * 512 + 512]
                        S.op("dve", lambda e, b=b, xs=xs: e.scalar_tensor_tensor(out=xs, in0=bank(b), scalar=MODS[l][:, 16 + fo2: 17 + fo2], in1=xs,
                                                                             op0=ALU.mult, op1=ALU.add),
                             reads=[R("ps", b), R("MOD", l), R("x", fo2, tc)], writes=[R("x", fo2, tc)])
                step(("wout", l, fo2), f2)

        def ffn(l, hooks):
            def A(j, tc2):
                return UO[:, j * 1024 + tc2 * 512: j * 1024 + tc2 * 512 + 512]
            M_ = MODS[l]
            for half in range(2):
                for jj in range(11):
                    def f(w, jj=jj, half=half):
                        for j2 in range(2):
                            j = 2 * jj + j2
                            for tc2 in range(2):
                                tc = 2 * half + tc2
                                bG = nbank(); bU = nbank()
                                for (b, c0) in [(bG, j2 * 2048), (bU, j2 * 2048 + 1024)]:
                                    for kc in range(KC):
                                        mm(bank(b), Wap(w, c0 + kc * 128, 128), hsl(kc, tc * 512, 512), kc == 0, kc == KC - 1,
                                           [R("W", w), R("h", kc, tc)], [R("ps", b)])
                                th = ntf()
                                S.op("act", lambda e: e.activation(out=TF[th][:, 0:512], in_=bank(bG), func=AF.Tanh, scale=0.5),
                                     reads=[R("ps", bG)], writes=[R("TF", th)])
                                S.op("dve", lambda e: e.scalar_tensor_tensor(
                                    out=TF[th][:, 0:512], in0=TF[th][:, 0:512], scalar=1.0, in1=bank(bG), op0=ALU.add, op1=ALU.mult),
                                    reads=[R("TF", th), R("ps", bG)], writes=[R("TF", th)])
                                S.op("dve", lambda e: e.tensor_tensor(
                                    out=A(j, tc2), in0=TF[th][:, 0:512], in1=bank(bU), op=ALU.mult),
                                    reads=[R("TF", th), R("ps", bU)], writes=[R("A", j, tc2)])
                    step(("gu", l, jj), f)
                    for hk in hooks.get((half, "gu", jj), []):
                        hk()
                for fo in range(8):
                    def f2(w, fo=fo, half=half):
                        for tc2 in range(2):
                            tc = 2 * half + tc2
                            b = nbank()
                            for j in range(NJ):
                                mm(bank(b), Wap(w, j * 128, 128), A(j, tc2), j == 0, j == NJ - 1, [R("W", w), R("A", j, tc2)], [R("ps", b)])
                            xs = xT[:, fo * S_ + tc * 512: fo * S_ + tc * 512 + 512]
                            S.op("dve", lambda e: e.scalar_tensor_tensor(out=xs, in0=bank(b), scalar=M_[:, 40 + fo: 41 + fo], in1=xs,
                                                                         op0=ALU.mult, op1=ALU.add),
                                 reads=[R("ps", b), R("MOD", l), R("x", fo, tc)], writes=[R("x", fo, tc)])
                    step(("down", l, fo), f2)
                    for hk in hooks.get((half, "down", fo), []):
                        hk()

        def bar():
            step(None, lambda _: S.barrier())

        def final_tc(tc):
            def f(_):
                t0 = tc * 512
                b = nbank()
                for kc in range(KC):
                    ti = ntb()
                    S.op("act", lambda e: e.activation(out=TB[ti][:, 0:512], in_=xT[:, kc * S_ + t0: kc * S_ + t0 + 512], func=AF.Square),
                         reads=[R("x", kc, tc)], writes=[R("TB", ti)])
                    mm(bank(b), ONES[:, :], TB[ti][:, 0:512], kc == 0, kc == KC - 1, [R("ONES"), R("TB", ti)], [R("ps", b)])
                rt = ntf()
                rstd_from_bank(b, float(D_), rt)
                for kc in range(KC):
                    t2 = ntf()
                    if t2 == rt:
                        t2 = ntf()
                    S.op("dve", lambda e: e.scalar_tensor_tensor(
                        out=TF[t2][:, 0:512], in0=xT[:, kc * S_ + t0: kc * S_ + t0 + 512], scalar=vcol("gfin", kc),
                        in1=TF[rt][:, 0:512], op0=ALU.mult, op1=ALU.mult),
                        reads=[R("x", kc, tc), R("VEC"), R("TF", rt)], writes=[R("TF", t2)])
                    S.dma("sp", lambda e: e.dma_start(out=yout[kc * 128:(kc + 1) * 128, t0:t0 + 512], in_=TF[t2][:, 0:512]),
                          reads=[R("TF", t2)], writes=[], sem=dsem_out[t2])
            step(None, f)

        for i in range(4):
            ada_step(0, i)
        ada_fin_m(0)
        for tc in range(NTC):
            norm_tc(0, 48, 0, tc)
        for l in range(L_):
            if l == 0:
                extra_q.extend([(lambda i=i: ada_step(0, i)) for i in range(4, 12)])
            na_branch(l)
            bar()
            mla_branch(l)
            bar()
            last = (l + 1 == L_)

            def before_attn(l=l, last=last):
                flush_extras()
                if not last:
                    extra_q.extend([(lambda i=i: ada_step(l + 1, i)) for i in range(12)])

            def after_attn(l=l, last=last):
                flush_extras()
                if not last:
                    ada_fin_m(l + 1)
                    ada_fin_rest(l + 1)
            gqa_branch(l, before_attn, after_attn)
            bar()
            if l == 0:
                ada_fin_rest(0)
            merge(l)
            bar()
            norm_tc(l, 56, 24, 0)
            norm_tc(l, 56, 24, 1)

            def nxt(tc, l=l, last=last):
                if last:
                    final_tc(tc)
                else:
                    norm_tc(l + 1, 48, 0, tc)
            hooks = {
                (0, "gu", 2): [lambda l=l: norm_tc(l, 56, 24, 2)],
                (0, "gu", 5): [lambda l=l: norm_tc(l, 56, 24, 3)],
                (1, "gu", 2): [lambda: nxt(0)],
                (1, "gu", 5): [lambda: nxt(1)],
            }
            ffn(l, hooks)
            bar()
            nxt(2)
            nxt(3)
        run_steps()
        S.emit()
    return nc


_CACHE = {}


def kernel(**inputs):
    inp = {k: np.asarray(v) for k, v in inputs.items()}
    B = inp["x"].shape[0]
    wp = pack_weights(inp)
    tab = rope_tables()
    nab = na_bias_tables(inp["na_rpb"]).reshape(L_ * 5, 128, 2560)
    in_maps = []
    for b in range(B):
        in_maps.append({
            "xT": np.ascontiguousarray(inp["x"][b].T),
            "wpk": wp,
            "vec": pack_vec(inp, b),
            "tab": tab,
            "nab": nab,
            "ident": np.eye(128, dtype=np.float32),
        })
    nc = build_program()
    res = run_bass_kernel_spmd(nc, in_maps, core_ids=list(range(B)))
    out = np.stack([np.ascontiguousarray(r["yT"].T) for r in res.results], axis=0)
    return out.astype(np.float32)
```
